# Optimizing a Trainium2 kernel written in Bass

```python
import math
import jax, jax.numpy as jnp
from jax import lax
import numpy as np

D_MODEL = 2048
BATCH = 2
SEQ = 4096
DEPTH = 4
DEC_BATCH = 8
DEC_SEQ = 8
PAST_LEN = 16384
PAGE_SIZE = 128

HEAD_DIM = 128
N_HEADS = D_MODEL // HEAD_DIM
HGRN_HEADS = N_HEADS // 2
MOBA_HEADS = N_HEADS - HGRN_HEADS
HGRN_DK = HEAD_DIM
HGRN_DV = HEAD_DIM
HGRN_W = HGRN_HEADS * HGRN_DK
MOBA_W = MOBA_HEADS * HEAD_DIM
MIX_W = HGRN_HEADS * HGRN_DV + MOBA_W
MOBA_BLOCK = 256
MOBA_TOPK = 3
MOBA_Q_CHUNK = 16
GLA_HEADS = 4
GLA_DK = D_MODEL // 2 // GLA_HEADS
GLA_DV = D_MODEL // GLA_HEADS
GLA_KW = GLA_HEADS * GLA_DK
GLA_VW = GLA_HEADS * GLA_DV
GLA_GATE_RANK = 16
GLA_GATE_NORM = 16.0
D_FF = ((8 * D_MODEL // 3 + 255) // 256) * 256
REC_CHUNK = 16
N_EVEN = (DEPTH + 1) // 2
N_ODD = DEPTH // 2
EVEN_IN = 4 * HGRN_W + 3 * MOBA_W
ODD_IN = 2 * GLA_KW + 2 * GLA_VW + GLA_GATE_RANK
NORM_EPS = 1e-6

kernel_name = "hgrn2_moba_gla_macaron_step"


def rms_norm(x, w):
    xf = x.astype(jnp.float32)
    y = xf * lax.rsqrt(jnp.mean(xf * xf, axis=-1, keepdims=True) + NORM_EPS)
    return (y * w.astype(jnp.float32)).astype(x.dtype)


def swiglu(x, wg, wu, wd):
    return (jax.nn.silu(x @ wg) * (x @ wu)) @ wd


def gated_linear_recurrence(q, k, v, log_f, s0):
    B, L, H, dk = q.shape
    dv = v.shape[-1]
    c = REC_CHUNK
    n = -(-L // c)
    pad = n * c - L
    f32 = jnp.float32

    def prep(a):
        a = jnp.pad(a.astype(f32), ((0, 0), (0, pad), (0, 0), (0, 0)))
        return a.reshape(B, n, c, H, a.shape[-1]).transpose(1, 0, 3, 2, 4)

    causal = jnp.tril(jnp.ones((c, c), bool))[:, :, None]

    def step(S, inp):
        qb, kb, vb, gb = inp
        b = jnp.cumsum(gb, axis=2)
        b_last = b[:, :, -1:, :]
        inter = jnp.einsum('bhtk,bhkv->bhtv', qb * jnp.exp(b), S)
        diff = jnp.where(causal, b[:, :, :, None, :] - b[:, :, None, :, :], -jnp.inf)
        att = jnp.einsum('bhtk,bhsk,bhtsk->bhts', qb, kb, jnp.exp(diff))
        intra = jnp.einsum('bhts,bhsv->bhtv', att, vb)
        S_new = S * jnp.exp(b_last[:, :, 0, :])[..., None] + jnp.einsum('bhsk,bhsv->bhkv', kb * jnp.exp(b_last - b), vb)
        return S_new, inter + intra

    S, o = lax.scan(step, s0.astype(f32), (prep(q), prep(k), prep(v), prep(log_f)))
    o = o.transpose(1, 0, 3, 2, 4).reshape(B, n * c, H, dv)[:, :L]
    return o.astype(q.dtype), S.astype(q.dtype)


def hgrn2_mix(q_raw, f_raw, i_raw, g_raw, lb, norm_w, s0):
    B, L, _ = q_raw.shape
    heads = lambda a: a.reshape(B, L, HGRN_HEADS, -1)
    q = jax.nn.silu(heads(q_raw)) * (HGRN_DK ** -0.5)
    f = lb + (1.0 - lb) * jax.nn.sigmoid(f_raw.astype(jnp.float32))
    o, S = gated_linear_recurrence(q, heads(1.0 - f), heads(i_raw), heads(jnp.log(f)), s0)
    o = rms_norm(o, norm_w) * jax.nn.silu(heads(g_raw))
    return o.reshape(B, L, HGRN_HEADS * HGRN_DV), S


def moba_attend(q, k_all, v_all, q_pos0):
    B, Lq, H, dh = q.shape
    T = k_all.shape[1]
    nb = -(-T // MOBA_BLOCK)
    padT = nb * MOBA_BLOCK - T
    blocks = lambda a: jnp.pad(a, ((0, 0), (0, padT), (0, 0), (0, 0))).reshape(B, nb, MOBA_BLOCK, H, dh).transpose(0, 3, 1, 2, 4)
    kb, vb = blocks(k_all), blocks(v_all)
    k_mean = jnp.mean(kb.astype(jnp.float32), axis=3)
    topk = min(MOBA_TOPK, nb)
    qc = math.gcd(Lq, MOBA_Q_CHUNK)
    nq = Lq // qc
    q_chunks = q.reshape(B, nq, qc, H, dh).transpose(1, 0, 2, 3, 4)
    pos_chunks = (q_pos0 + jnp.arange(Lq, dtype=jnp.int32)).reshape(nq, qc)
    bi = jnp.arange(B)[:, None, None, None]
    hi = jnp.arange(H)[None, :, None, None]
    blk_ids = jnp.arange(nb, dtype=jnp.int32)
    scale = dh ** -0.5

    def one_chunk(args):
        qq, pos = args
        own = pos // MOBA_BLOCK
        scores = jnp.einsum('bqhd,bhnd->bhqn', qq.astype(jnp.float32), k_mean)
        past = blk_ids[None, :] < own[:, None]
        scores = jnp.where(past[None, None], scores, -jnp.inf)
        _, sel = lax.top_k(scores, topk)
        own_b = jnp.broadcast_to(own[None, None, :, None], (B, H, qc, 1))
        idx = jnp.concatenate([sel, own_b], axis=-1)
        valid = jnp.concatenate([sel < own_b, jnp.ones((B, H, qc, 1), bool)], axis=-1)
        kg = kb[bi, hi, idx]
        vg = vb[bi, hi, idx]
        key_pos = idx[..., None] * MOBA_BLOCK + jnp.arange(MOBA_BLOCK, dtype=jnp.int32)
        mask = valid[..., None] & (key_pos <= pos[None, None, :, None, None])
        logits = jnp.einsum('bqhd,bhqnkd->bhqnk', qq, kg).astype(jnp.float32) * scale
        logits = jnp.where(mask, logits, -jnp.inf)
        p = jax.nn.softmax(logits.reshape(B, H, qc, -1), axis=-1).reshape(logits.shape)
        return jnp.einsum('bhqnk,bhqnkd->bqhd', p.astype(vg.dtype), vg)

    out = lax.map(one_chunk, (q_chunks, pos_chunks))
    return out.transpose(1, 0, 2, 3, 4).reshape(B, Lq, H, dh)


def even_mixer(h, w_in, w_out, lb, hgrn_norm, s0, k_past, v_past, q_pos0):
    B, L, _ = h.shape
    proj = h @ w_in
    cuts = [HGRN_W, 2 * HGRN_W, 3 * HGRN_W, 4 * HGRN_W, 4 * HGRN_W + MOBA_W, 4 * HGRN_W + 2 * MOBA_W]
    hq, hf, hi_, hg, mq, mk, mv = jnp.split(proj, cuts, axis=-1)
    o_h, S = hgrn2_mix(hq, hf, hi_, hg, lb, hgrn_norm, s0)
    heads = lambda a: a.reshape(B, L, MOBA_HEADS, HEAD_DIM)
    k_new, v_new = heads(mk), heads(mv)
    k_all = k_new if k_past is None else jnp.concatenate([k_past, k_new], axis=1)
    v_all = v_new if v_past is None else jnp.concatenate([v_past, v_new], axis=1)
    o_m = moba_attend(heads(mq), k_all, v_all, q_pos0).reshape(B, L, MOBA_W)
    y = jnp.concatenate([o_h, o_m], axis=-1) @ w_out
    return y, S, k_new, v_new


def gla_mixer(h, w_in, w_gate_up, b_gate, norm_w, w_out, s0):
    B, L, _ = h.shape
    proj = h @ w_in
    q, k, v, r, a = jnp.split(proj, [GLA_KW, 2 * GLA_KW, 2 * GLA_KW + GLA_VW, 2 * GLA_KW + 2 * GLA_VW], axis=-1)
    log_a = jax.nn.log_sigmoid((a @ w_gate_up + b_gate).astype(jnp.float32)) / GLA_GATE_NORM
    hk = lambda t: t.reshape(B, L, GLA_HEADS, GLA_DK)
    hv = lambda t: t.reshape(B, L, GLA_HEADS, GLA_DV)
    o, S = gated_linear_recurrence(hk(q) * (GLA_DK ** -0.5), hk(k), hv(v), hk(log_a), s0)
    o = rms_norm(o, norm_w) * jax.nn.silu(hv(r))
    return o.reshape(B, L, GLA_VW) @ w_out, S


def run_trunk(x, hgrn_s0, gla_s0, k_past, v_past, q_pos0, lb, norm_w, ffn_gate, ffn_up, ffn_down,
              even_w_in, even_w_out, hgrn_norm_w, gla_w_in, gla_w_gate_up, gla_b_gate, gla_norm_w,
              gla_w_out, final_norm_w):
    ks, vs, hs, gs = [], [], [], []
    for l in range(DEPTH):
        x = x + 0.5 * swiglu(rms_norm(x, norm_w[l, 0]), ffn_gate[l, 0], ffn_up[l, 0], ffn_down[l, 0])
        h = rms_norm(x, norm_w[l, 1])
        if l % 2 == 0:
            e = l // 2
            kp = None if k_past is None else k_past[e]
            vp = None if v_past is None else v_past[e]
            y, S, kn, vn = even_mixer(h, even_w_in[e], even_w_out[e], lb[e], hgrn_norm_w[e], hgrn_s0[e], kp, vp, q_pos0)
            ks.append(kn)
            vs.append(vn)
            hs.append(S)
        else:
            o = l // 2
            y, S = gla_mixer(h, gla_w_in[o], gla_w_gate_up[o], gla_b_gate[o], gla_norm_w[o], gla_w_out[o], gla_s0[o])
            gs.append(S)
        x = x + y
        x = x + 0.5 * swiglu(rms_norm(x, norm_w[l, 2]), ffn_gate[l, 1], ffn_up[l, 1], ffn_down[l, 1])
    return rms_norm(x, final_norm_w), ks, vs, hs, gs


def setup_inputs(seed: int = 0) -> dict:
    key = jax.random.key(seed)
    ks = jax.random.split(key, 24)
    f32 = jnp.float32
    nrm = lambda k, shape, scale: jax.random.normal(k, shape, f32) * scale
    n_pages = PAST_LEN // PAGE_SIZE
    n_used = DEC_BATCH * n_pages
    n_pool = n_used + (n_used + 3) // 4
    page_table = jax.random.permutation(ks[6], n_pool)[:n_used].reshape(DEC_BATCH, n_pages).astype(jnp.int32)
    return {
        'x_prompt': nrm(ks[0], (BATCH, SEQ, D_MODEL), 1.0),
        'x_sample': nrm(ks[1], (DEC_BATCH, DEC_SEQ, D_MODEL), 1.0),
        'cache_k': nrm(ks[2], (N_EVEN, n_pool, PAGE_SIZE, MOBA_HEADS, HEAD_DIM), 1.0),
        'cache_v': nrm(ks[3], (N_EVEN, n_pool, PAGE_SIZE, MOBA_HEADS, HEAD_DIM), 1.0),
        'state_hgrn': nrm(ks[4], (N_EVEN, DEC_BATCH, HGRN_HEADS, HGRN_DK, HGRN_DV), 1.0),
        'state_gla': nrm(ks[5], (N_ODD, DEC_BATCH, GLA_HEADS, GLA_DK, GLA_DV), 1.0),
        'page_table': page_table,
        'norm_w': 1.0 + nrm(ks[7], (DEPTH, 3, D_MODEL), 0.02),
        'ffn_gate': nrm(ks[8], (DEPTH, 2, D_MODEL, D_FF), D_MODEL ** -0.5),
        'ffn_up': nrm(ks[9], (DEPTH, 2, D_MODEL, D_FF), D_MODEL ** -0.5),
        'ffn_down': nrm(ks[10], (DEPTH, 2, D_FF, D_MODEL), D_FF ** -0.5),
        'even_w_in': nrm(ks[11], (N_EVEN, D_MODEL, EVEN_IN), D_MODEL ** -0.5),
        'even_w_out': nrm(ks[12], (N_EVEN, MIX_W, D_MODEL), MIX_W ** -0.5),
        'hgrn_lb_logits': nrm(ks[13], (N_EVEN, HGRN_W), 0.5),
        'hgrn_norm_w': 1.0 + nrm(ks[14], (N_EVEN, HGRN_DV), 0.02),
        'gla_w_in': nrm(ks[15], (N_ODD, D_MODEL, ODD_IN), D_MODEL ** -0.5),
        'gla_w_gate_up': nrm(ks[16], (N_ODD, GLA_GATE_RANK, GLA_KW), GLA_GATE_RANK ** -0.5),
        'gla_b_gate': nrm(ks[17], (N_ODD, GLA_KW), 0.1),
        'gla_norm_w': 1.0 + nrm(ks[18], (N_ODD, GLA_DV), 0.02),
        'gla_w_out': nrm(ks[19], (N_ODD, GLA_VW, D_MODEL), GLA_VW ** -0.5),
        'final_norm_w': 1.0 + nrm(ks[20], (D_MODEL,), 0.02),
    }


def reference(x_prompt, x_sample, cache_k, cache_v, state_hgrn, state_gla, page_table, norm_w, ffn_gate,
              ffn_up, ffn_down, even_w_in, even_w_out, hgrn_lb_logits, hgrn_norm_w, gla_w_in, gla_w_gate_up,
              gla_b_gate, gla_norm_w, gla_w_out, final_norm_w):
    lb = jnp.cumsum(jax.nn.softmax(hgrn_lb_logits.astype(jnp.float32), axis=0), axis=0)
    lb = lb - lb[0:1]
    weights = (norm_w, ffn_gate, ffn_up, ffn_down, even_w_in, even_w_out, hgrn_norm_w, gla_w_in,
               gla_w_gate_up, gla_b_gate, gla_norm_w, gla_w_out, final_norm_w)

    Bp = x_prompt.shape[0]
    h0 = [jnp.zeros((Bp, HGRN_HEADS, HGRN_DK, HGRN_DV), x_prompt.dtype) for _ in range(N_EVEN)]
    g0 = [jnp.zeros((Bp, GLA_HEADS, GLA_DK, GLA_DV), x_prompt.dtype) for _ in range(N_ODD)]
    y_prompt, kp, vp, hp, gp = run_trunk(x_prompt, h0, g0, None, None, 0, lb, *weights)

    Bs = x_sample.shape[0]
    past_len = page_table.shape[1] * PAGE_SIZE
    k_past = [cache_k[e][page_table].reshape(Bs, past_len, MOBA_HEADS, HEAD_DIM) for e in range(N_EVEN)]
    v_past = [cache_v[e][page_table].reshape(Bs, past_len, MOBA_HEADS, HEAD_DIM) for e in range(N_EVEN)]
    hs0 = [state_hgrn[e] for e in range(N_EVEN)]
    gs0 = [state_gla[o] for o in range(N_ODD)]
    y_sample, ksn, vsn, hsn, gsn = run_trunk(x_sample, hs0, gs0, k_past, v_past, past_len, lb, *weights)

    k_prompt = jnp.stack([k.reshape(Bp, -1, PAGE_SIZE, MOBA_HEADS, HEAD_DIM) for k in kp])
    v_prompt = jnp.stack([v.reshape(Bp, -1, PAGE_SIZE, MOBA_HEADS, HEAD_DIM) for v in vp])
    hgrn_prompt = jnp.stack(hp)
    gla_prompt = jnp.stack(gp)
    k_sample = jnp.stack(ksn)
    v_sample = jnp.stack(vsn)
    hgrn_sample = jnp.stack(hsn)
    gla_sample = jnp.stack(gsn)
    return (y_prompt, y_sample, k_prompt, v_prompt, hgrn_prompt, gla_prompt, k_sample, v_sample, hgrn_sample, gla_sample)
```

```python
import numpy as np
from contextlib import ExitStack
import concourse.bass as bass
import concourse.mybir as mybir
from concourse.bass_utils import run_bass_kernel_spmd

F32 = mybir.dt.float32
BF16 = mybir.dt.bfloat16
I32 = mybir.dt.int32
AF = mybir.ActivationFunctionType
ALU = mybir.AluOpType
AX = mybir.AxisListType

D = 2048
NCH = 16
T = 1032
TP = 1024
TS = 8
TG = [(0, 344), (344, 344), (688, 344)]
DFF = 5632
NG = DFF // 256
EPS = 1e-6
DEPTH = 4
NS = 4
ARENA = 62 * 1024
TT = [(96 * i, 96) for i in range(10)] + [(960, 64), (1024, 8)]
NTT = len(TT)
NDS = 24
SAMPLE_MOBA_DEFAULT = True


class Tr:
    def __init__(self, nc, st):
        self.nc = nc
        self.eng = {'pe': nc.tensor, 'dve': nc.vector, 'act': nc.scalar, 'pool': nc.gpsimd, 'sp': nc.sync}
        self.sem = {k: st.enter_context(nc.semaphore('s_' + k)) for k in self.eng}
        self.cnt = {k: 0 for k in self.eng}
        self.seen = {k: {} for k in self.eng}
        self.lastw = {}
        self.readers = {}
        self.dsem = [st.enter_context(nc.semaphore('d%d' % i)) for i in range(NDS)]
        self.dcnt = [0] * NDS
        self.dnext = 0
        self.psem = [st.enter_context(nc.semaphore('p%d' % i)) for i in range(8)]
        self.pcnt = [0] * 8
        self.pnext = 0
        self.xsem = {}
        self.xcnt = {}
        self.st = st
        self.outtoks = []

    def semh(self, key):
        if isinstance(key, str):
            return self.sem[key]
        if key[0] == 'd':
            return self.dsem[key[1]]
        if key[0] == 'p':
            return self.psem[key[1]]
        return self.xsem[key]

    def _need(self, e, deps):
        for key, val in deps.items():
            if key == e and e in ('pe', 'sp'):
                continue
            if self.seen[e].get(key, 0) >= val:
                continue
            self.eng[e].wait_ge(self.semh(key), val)
            self.seen[e][key] = val

    def deps_for(self, reads, writes):
        deps = {}
        for r in reads:
            w = self.lastw.get(r)
            if w is not None and deps.get(w[0], 0) < w[1]:
                deps[w[0]] = w[1]
        for r in writes:
            w = self.lastw.get(r)
            if w is not None and deps.get(w[0], 0) < w[1]:
                deps[w[0]] = w[1]
            for rd in self.readers.get(r, ()):
                if deps.get(rd[0], 0) < rd[1]:
                    deps[rd[0]] = rd[1]
        return deps

    def _record(self, tok, reads, writes):
        for r in reads:
            self.readers.setdefault(r, []).append(tok)
        for r in writes:
            self.lastw[r] = tok
            self.readers[r] = []

    def op(self, e, fn, reads=(), writes=()):
        self._need(e, self.deps_for(reads, writes))
        ins = fn()
        self.cnt[e] += 1
        ins.then_inc(self.sem[e], 1)
        self._record((e, self.cnt[e]), reads, writes)
        return ins

    def group(self, e, fns, reads=(), writes=()):
        self._need(e, self.deps_for(reads, writes))
        ins = None
        for fn in fns:
            ins = fn()
        self.cnt[e] += 1
        ins.then_inc(self.sem[e], 1)
        self._record((e, self.cnt[e]), reads, writes)

    def newsem(self, name):
        key = ('x', name)
        self.xsem[key] = self.st.enter_context(self.nc.semaphore('x_' + name))
        self.xcnt[key] = 0
        return key

    def dma(self, q, out, in_, reads=(), writes=(), semkey=None, is_out=False, fn=None, inc=16):
        self._need(q, self.deps_for(reads, writes))
        if semkey is None and q == 'pool':
            i = self.pnext
            self.pnext = (i + 1) % 8
            key = ('p', i)
            if self.pcnt[i] > 0:
                self._need(q, {key: self.pcnt[i]})
            self.pcnt[i] += inc
            val = self.pcnt[i]
        elif semkey is None:
            i = self.dnext
            self.dnext = (i + 1) % NDS
            key = ('d', i)
            if self.dcnt[i] > 0:
                self._need(q, {key: self.dcnt[i]})
            self.dcnt[i] += inc
            val = self.dcnt[i]
        else:
            key = semkey
            if self.xcnt[key] > 0:
                self._need(q, {key: self.xcnt[key]})
            self.xcnt[key] += inc
            val = self.xcnt[key]
        if fn is None:
            ins = self.eng[q].dma_start(out=out, in_=in_)
        else:
            ins = fn()
        if inc == 16:
            ins.then_inc(self.semh(key), 16)
        else:
            ins.then_inc(self.semh(key))
        tok = (key, val)
        self._record(tok, reads, writes)
        if is_out:
            self.outtoks.append(tok)
        return tok

    def barrier(self):
        deps = {}
        for k, v in self.cnt.items():
            if v > 0:
                deps[k] = v
        for i in range(NDS):
            if self.dcnt[i] > 0:
                deps[('d', i)] = self.dcnt[i]
        for i in range(8):
            if self.pcnt[i] > 0:
                deps[('p', i)] = self.pcnt[i]
        for k, v in self.xcnt.items():
            if v > 0:
                deps[k] = v
        for e in self.eng:
            self._need(e, deps)

    def epoch(self):
        switched = False
        for e in ('pe', 'dve', 'act'):
            if self.cnt[e] > 12000:
                self.nep = getattr(self, 'nep', 0) + 1
                self.sem[e] = self.st.enter_context(self.nc.semaphore('s_%s_%d' % (e, self.nep)))
                self.cnt[e] = 0
                for k in self.seen:
                    self.seen[k].pop(e, None)
                switched = True
        if switched:
            self.lastw = {}
            self.readers = {}

    def finish(self):
        deps = {}
        for key, val in self.outtoks:
            deps[key] = max(deps.get(key, 0), val)
        self._need('sp', deps)
        self.barrier()


class Ctx:
    pass


class _Lazy(dict):
    def __init__(self, d):
        super().__init__(d)

    def __getitem__(self, k):
        v = dict.get(self, k)
        if v is None:
            return _Zero()
        return v


class _Zero:
    def __getitem__(self, k):
        return self

    def reshape(self, *a):
        return self

    def transpose(self, *a):
        return self

    def astype(self, *a):
        return self

    @property
    def T(self):
        return self


def weight_seq(flags):
    seq = []

    def ffn(l, i):
        pend = None
        for g in range(NG):
            seq.append(('gate', l, i, g))
            seq.append(('up', l, i, g))
            if pend is not None:
                seq.append(('down', l, i, pend))
            pend = g
        seq.append(('down', l, i, pend))

    def gla(o):
        for h in range(4):
            seq.append(('glain', o, h))
            seq.append(('glain', o, 4 + h))
            seq.append(('glain', o, 8 + 2 * h))
            seq.append(('glain', o, 9 + 2 * h))
            seq.append(('glain', o, 16 + 2 * h))
            seq.append(('glain', o, 17 + 2 * h))
            seq.append(('glaout', o, 2 * h))
            seq.append(('glaout', o, 2 * h + 1))

    def even(e):
        for hp in range(4 if flags.get('ev_hgrn', True) else 0):
            for blk in (0, 4, 8, 12):
                seq.append(('evin', e, blk + hp))
            seq.append(('evout', e, hp))
        for mp in range(4 if flags.get('ev_moba', True) else 0):
            for blk in (16, 20, 24):
                seq.append(('evin', e, blk + mp))
        for mp in range(4 if flags.get('ev_moba', True) else 0):
            seq.append(('evout', e, 4 + mp))

    for layer in flags.get('prog', DEFAULT_PROG):
        for q in range(flags.get('nq', 4)):
            for (kind, l) in layer:
                if kind == 'even':
                    even(l // 2)
                if kind == 'ffn0':
                    ffn(l, 0)
                elif kind == 'ffn1':
                    ffn(l, 1)
                elif kind == 'gla':
                    gla(l // 2)
    return seq


_F = {'ffn0', 'ffn1'}
_G = {'gla'}
_E = {'even'}
IN_GROUP = {'ffn_gate': _F, 'ffn_up': _F, 'ffn_down': _F,
            'gla_w_in': _G, 'gla_w_gate_up': _G, 'gla_b_gate': _G, 'gla_w_out': _G, 'sgla': _G,
            'even_w_in': _E, 'even_w_out': _E, 'lbrow': _E, 'lbT': _E, 'shgrn': _E,
            'cache_k': _E, 'cache_v': _E, 'ptab': _E}
DEFAULT_PROG = []
for _l in range(DEPTH):
    DEFAULT_PROG.append([('ffn0', _l), ('gla' if _l % 2 else 'even', _l), ('ffn1', _l)])


def build(flags):
    nc = bass.Bass("TRN2", target_bir_lowering=False)
    K = Ctx()
    K.nc = nc
    K.flags = flags
    dr = {}

    def din(name, shape, dt=F32):
        grp = IN_GROUP.get(name)
        if grp is not None and not (kinds & grp):
            return None
        dr[name] = nc.dram_tensor(name, list(shape), dt, kind="ExternalInput").ap()
        return dr[name]

    def dout(name, shape, dt=F32):
        dr[name] = nc.dram_tensor(name, list(shape), dt, kind="ExternalOutput").ap()
        return dr[name]

    kinds = set(k for layer in flags.get('prog', DEFAULT_PROG) for k, _ in layer)
    din('xp', [4 * TP, D])
    din('xs', [TS, D])
    din('normT', [128, 13 * NCH])
    din('ffn_gate', [DEPTH, 2, D, DFF])
    din('ffn_up', [DEPTH, 2, D, DFF])
    din('ffn_down', [DEPTH, 2, DFF, D])
    din('gla_w_in', [2, D, 6160])
    din('gla_w_gate_up', [2, 16, 1024])
    din('gla_b_gate', [2, 1, 1024])
    din('gla_w_out', [2, D, D])
    din('gnormT', [128, 8])
    din('even_w_in', [2, D, 7168])
    din('even_w_out', [2, D, D])
    din('lbrow', [128, 2048])
    din('lbT', [128, 16])
    din('hnormT', [128, 2])
    din('shgrn', [2, 8, 128, 128])
    if flags.get('sample_moba', SAMPLE_MOBA_DEFAULT):
        din('cache_k', [2, 1280 * 128, 1024])
        din('cache_v', [2, 1280 * 128, 1024])
        din('ptab', [128, 128], I32)
    dout('hgrn_p', [2, 8, 128, 128])
    dout('hgrn_s', [2, 8, 128, 128])
    dout('kp', [2, 4 * TP, 1024])
    dout('vp', [2, 4 * TP, 1024])
    dout('ks', [2, TS, 1024])
    dout('vs', [2, TS, 1024])
    KTd = nc.dram_tensor('KTd', [8, 128, 4 * TP], BF16).ap()
    din('sgla', [2, 4, 256, 512])
    dout('gla_p', [2, 4, 256, 512])
    dout('gla_s', [2, 4, 256, 512])
    Xd = nc.dram_tensor('Xd', [4, 128, NCH * TP], F32).ap()
    Xs = nc.dram_tensor('Xs', [128, NCH * TS], F32).ap()
    dout('yp', [4 * TP, D])
    dout('ys', [TS, D])
    if flags.get('dbg'):
        dout('dbg_h', [128, NCH * T], BF16)
        dout('dbg_a', [128, 2 * T], BF16)
        dout('dbg_x', [128, NCH * T], F32)
        dout('dbg_p', [128, 3 * 344], F32)

    with ExitStack() as st:
        tr = Tr(nc, st)
        K.tr = tr

        def sb(name, shape, dt):
            return st.enter_context(nc.sbuf_tensor('sb_' + name, list(shape), dt))

        xT = sb('xT', [128, NCH, T], F32)
        hT = sb('hT', [128, NCH, T], BF16)
        ring = [sb('ring%d' % i, [128, 4096], BF16) for i in range(NS)]
        arena = sb('arena', [128, ARENA // 2], BF16)
        rmask = sb('rmask', [128, T], F32)
        cmask = sb('cmask', [128, 32], F32)
        revtri = sb('revtri', [128, 128], F32)
        onec = sb('onec', [128, 1], F32)
        gnw = sb('gnw', [128, 8], F32)
        hnw = sb('hnw', [128, 2], F32)
        kmT = sb('kmT', [128, 8, 16], BF16)
        esel = sb('esel', [32, 16, 128], BF16)
        mskA = sb('mskA', [128, 256], F32)
        mskB = sb('mskB', [128, 256], F32)
        negc = sb('negc', [128, 24], F32)

        class Arena:
            def __init__(self):
                self.off = 0

            def reset(self):
                tr.barrier()
                tr.epoch()
                self.off = 0

            def __call__(self, shape, dt):
                n = 1
                for x in shape:
                    n *= x
                nb = n * (4 if dt == F32 else 2)
                nb = (nb + 31) // 32 * 32
                assert self.off + nb <= ARENA, (self.off, nb)
                v = arena[:, self.off // 2:(self.off + nb) // 2]
                self.off += nb
                if dt == F32:
                    v = v.bitcast(F32)
                v = v[:, 0:n]
                if len(shape) == 2:
                    v = v.rearrange("p (a b) -> p a b", a=shape[0])
                elif len(shape) == 3:
                    v = v.rearrange("p (a b c) -> p a b c", a=shape[0], b=shape[1])
                return v

        AR = Arena()
        aT = [None, None]
        stmp = [None, None, None]
        io = [None, None]

        def alloc_ffn():
            AR.reset()
            for i in range(2):
                aT[i] = AR([2, T], BF16)
            for i in range(3):
                stmp[i] = AR([344], F32)

        def alloc_io():
            AR.reset()
            for i in range(2):
                io[i] = AR([D], F32)
        rstd = sb('rstd', [128, T], F32)
        nw = sb('nw', [128, 13 * NCH], F32)
        ones_bf = sb('ones_bf', [128, 128], BF16)
        ident = sb('ident', [128, 128], F32)
        epsc = sb('epsc', [128, 1], F32)
        ps = [st.enter_context(nc.psum_tensor('ps%d' % i, [128, 512], F32)) for i in range(8)]
        wsem = [tr.newsem('w%d' % i) for i in range(NS)]

        tr.op('pool', lambda: nc.gpsimd.memset(ones_bf[:], 1.0), writes=['ones'])
        tr.op('pool', lambda: nc.gpsimd.memset(ident[:], 0.0), writes=['ident'])
        tr.op('pool', lambda: nc.gpsimd.affine_select(out=ident[:], in_=ident[:], pattern=[[-1, 128]],
                                                      compare_op=ALU.not_equal, fill=1.0, base=0,
                                                      channel_multiplier=1),
              reads=['ident'], writes=['ident'])
        tr.dma('sp', nw[:], dr['normT'][:, :], writes=['nw'])
        tr.op('pool', lambda: nc.gpsimd.memset(epsc[:], EPS), writes=['epsc'])
        tr.op('pool', lambda: nc.gpsimd.memset(onec[:], 1.0), writes=['onec'])
        tr.op('pool', lambda: nc.gpsimd.memset(rmask[:], 1.0), writes=['rmask'])
        tr.op('pool', lambda: nc.gpsimd.memset(
            rmask[:, 0:1024].rearrange("p (j s) -> p j s", s=32)[:, :, 0:1], 0.0), reads=['rmask'], writes=['rmask'])
        tr.op('pool', lambda: nc.gpsimd.memset(rmask[:, 1024:1025], 0.0), reads=['rmask'], writes=['rmask'])
        tr.op('pool', lambda: nc.gpsimd.memset(cmask[:], 1.0), writes=['cmask'])
        tr.op('pool', lambda: nc.gpsimd.memset(revtri[:], 0.0), writes=['revtri'])
        for blk in range(4):
            b0 = blk * 32
            tr.op('pool', lambda b0=b0: nc.gpsimd.affine_select(
                out=cmask[b0:b0 + 32, :], in_=cmask[b0:b0 + 32, :], pattern=[[1, 32]], compare_op=ALU.is_ge,
                fill=0.0, base=0, channel_multiplier=-1), reads=['cmask'], writes=['cmask'])
            tr.op('pool', lambda b0=b0: nc.gpsimd.memset(revtri[b0:b0 + 32, b0:b0 + 32], 1.0),
                  reads=['revtri'], writes=['revtri'])
            tr.op('pool', lambda b0=b0: nc.gpsimd.affine_select(
                out=revtri[b0:b0 + 32, b0:b0 + 32], in_=revtri[b0:b0 + 32, b0:b0 + 32], pattern=[[-1, 32]],
                compare_op=ALU.is_gt, fill=0.0, base=0, channel_multiplier=1), reads=['revtri'], writes=['revtri'])
        tr.dma('sp', gnw[:], dr['gnormT'][:, :], writes=['gnw'])
        tr.dma('sp', hnw[:], dr['hnormT'][:, :], writes=['hnw'])
        tr.op('pool', lambda: nc.gpsimd.memset(negc[:], -1.0e30), writes=['negc'])
        tr.op('pool', lambda: nc.gpsimd.memset(esel[:], 1.0), writes=['esel'])
        tr.op('pool', lambda: nc.gpsimd.affine_select(
            out=esel[:], in_=esel[:], pattern=[[-1, 16], [0, 128]], compare_op=ALU.is_equal, fill=0.0, base=0,
            channel_multiplier=1), reads=['esel'], writes=['esel'])
        tr.op('pool', lambda: nc.gpsimd.memset(mskA[:], 1.0), writes=['mskA'])
        tr.op('pool', lambda: nc.gpsimd.affine_select(
            out=mskA[:, 0:128], in_=mskA[:, 0:128], pattern=[[1, 128]], compare_op=ALU.is_ge, fill=0.0, base=0,
            channel_multiplier=-1), reads=['mskA'], writes=['mskA'])
        tr.op('pool', lambda: nc.gpsimd.memset(mskB[:], 0.0), writes=['mskB'])
        tr.op('pool', lambda: nc.gpsimd.memset(mskB[:, 128:256], 1.0), reads=['mskB'], writes=['mskB'])
        tr.op('pool', lambda: nc.gpsimd.affine_select(
            out=mskB[:, 128:256], in_=mskB[:, 128:256], pattern=[[1, 128]], compare_op=ALU.is_ge, fill=0.0, base=0,
            channel_multiplier=-1), reads=['mskB'], writes=['mskB'])

        wseq = weight_seq(flags)
        wstate = {'issued': 0, 'used': 0, 'floor': 0, 'rel': set()}

        def wsrc(desc):
            kind = desc[0]
            if kind in ('gate', 'up'):
                _, l, i, g = desc
                src = dr['ffn_' + kind][l, i][:, g * 256:(g + 1) * 256].rearrange("(c p) f -> p c f", p=128)
                return src, 'cols'
            if kind == 'down':
                _, l, i, g = desc
                src = dr['ffn_down'][l, i][g * 256:(g + 1) * 256, :].rearrange("(j p) m -> p j m", p=128)
                return src, 'rows'
            if kind == 'glain':
                _, o, blk = desc
                src = dr['gla_w_in'][o][:, blk * 256:(blk + 1) * 256].rearrange("(c p) f -> p c f", p=128)
                return src, 'cols'
            if kind == 'evin':
                _, e, blk = desc
                src = dr['even_w_in'][e][:, blk * 256:(blk + 1) * 256].rearrange("(c p) f -> p c f", p=128)
                return src, 'cols'
            if kind == 'evout':
                _, e, rt = desc
                src = dr['even_w_out'][e][rt * 256:(rt + 1) * 256, :].rearrange("(j p) m -> p j m", p=128)
                return src, 'rows'
            if kind == 'glaout':
                _, o, rt = desc
                src = dr['gla_w_out'][o][rt * 256:(rt + 1) * 256, :].rearrange("(j p) m -> p j m", p=128)
                return src, 'rows'
            raise ValueError(kind)

        def wview(slot, lay):
            if lay == 'cols':
                return ring[slot][:].rearrange("p (c f) -> p c f", c=16)
            return ring[slot][:].rearrange("p (j m) -> p j m", j=2)

        def wissue():
            n = wstate['issued']
            desc = wseq[n]
            slot = n % NS
            src, lay = wsrc(desc)
            tr.dma('pool', wview(slot, lay), src, writes=[('w', slot)], semkey=wsem[slot])
            wstate['issued'] += 1

        def wpump():
            while wstate['issued'] < len(wseq) and wstate['issued'] - NS < wstate['floor']:
                wissue()

        def wnext(desc):
            n = wstate['used']
            assert wseq[n] == desc, (wseq[n], desc)
            wpump()
            assert wstate['issued'] > n
            wstate['used'] += 1
            slot = n % NS
            _, lay = wsrc(desc)
            return wview(slot, lay), ('w', slot), n

        def wrel(n):
            wstate['rel'].add(n)
            while wstate['floor'] in wstate['rel']:
                wstate['rel'].discard(wstate['floor'])
                wstate['floor'] += 1
            wpump()

        psrr = {'A': 0, 'B': 0}

        def psA():
            i = psrr['A'] % 4
            psrr['A'] += 1
            return ps[i], ('ps', i)

        def psB():
            i = 4 + psrr['B'] % 3
            psrr['B'] += 1
            return ps[i], ('ps', i)

        xkeys = [('xT', c) for c in range(NCH)]
        hkeys = [('hT', c) for c in range(NCH)]

        def load_x(q):
            alloc_io()
            for i in range(9):
                buf = io[i % 2]
                bkey = ('io', i % 2)
                if i < 8:
                    n = 128
                    tr.dma('sp', buf[:, :], dr['xp'][q * TP + i * 128:q * TP + (i + 1) * 128, :], writes=[bkey])
                else:
                    n = TS
                    tr.dma('sp', buf[0:TS, :], dr['xs'][:, :], writes=[bkey])
                for cg in range(4):
                    pt, pk = psA()
                    fns = []
                    for cc in range(4):
                        c = cg * 4 + cc
                        fns.append(lambda c=c, cc=cc, pt=pt, n=n, buf=buf: nc.tensor.transpose(
                            out=pt[:, cc * 128:cc * 128 + n], in_=buf[0:n, c * 128:(c + 1) * 128],
                            identity=ident[0:n, 0:n]))
                    for fn in fns:
                        tr.op('pe', fn, reads=[bkey, 'ident'], writes=[pk])
                    src = pt[:].rearrange("p (c n) -> p c n", c=4)[:, :, 0:n]
                    dst = xT[:, cg * 4:(cg + 1) * 4, i * 128:i * 128 + n]
                    tr.op('dve', lambda dst=dst, src=src: nc.vector.tensor_copy(out=dst, in_=src),
                          reads=[pk], writes=[('xT', cg * 4 + k) for k in range(4)])

        def compute_rstd(width):
            for c in range(NCH):
                tr.op('act', lambda c=c: nc.scalar.activation(out=hT[:, c, :], in_=xT[:, c, :], func=AF.Square),
                      reads=[('xT', c)], writes=[('hT', c)])
            for t, (t0, tn) in enumerate(TG):
                pt, pk = psB()
                fns = [(lambda c=c, pt=pt, t0=t0, tn=tn: nc.tensor.matmul(
                    pt[:, 0:tn], lhsT=ones_bf[:, :], rhs=hT[:, c, t0:t0 + tn], start=(c == 0), stop=(c == NCH - 1)))
                    for c in range(NCH)]
                tr.group('pe', fns, reads=hkeys + ['ones'], writes=[pk])
                tr.op('act', lambda pt=pt, t0=t0, tn=tn: nc.scalar.activation(
                    out=rstd[:, t0:t0 + tn], in_=pt[:, 0:tn], func=AF.Sqrt, bias=epsc[:, 0:1], scale=1.0 / width),
                    reads=[pk, 'epsc'], writes=[('rstd', t)])
                tr.op('dve', lambda t0=t0, tn=tn: nc.vector.reciprocal(
                    out=rstd[:, t0:t0 + tn], in_=rstd[:, t0:t0 + tn]), reads=[('rstd', t)], writes=[('rstd', t)])

        rkeys = [('rstd', t) for t in range(3)]

        def rmsnorm(nidx):
            compute_rstd(float(D))
            for c in range(NCH):
                tr.op('dve', lambda c=c: nc.vector.scalar_tensor_tensor(
                    out=hT[:, c, :], in0=xT[:, c, :], scalar=nw[:, nidx * NCH + c:nidx * NCH + c + 1],
                    in1=rstd[:, :], op0=ALU.mult, op1=ALU.mult),
                    reads=[('xT', c), 'nw'] + rkeys, writes=[('hT', c)])

        def ffn(l, i, nidx):
            alloc_ffn()
            rmsnorm(nidx)
            pend = None

            def down(g, l=l, i=i):
                wd, wk, wn = wnext(('down', l, i, g))
                a = aT[g % 2]
                for m in range(NCH):
                    for t, (t0, tn) in enumerate(TG):
                        pt, pk = psB()
                        fns = [(lambda j=j, pt=pt, m=m, t0=t0, tn=tn: nc.tensor.matmul(
                            pt[:, 0:tn], lhsT=wd[:, j, m * 128:(m + 1) * 128], rhs=a[:, j, t0:t0 + tn],
                            start=(j == 0), stop=(j == 1))) for j in range(2)]
                        tr.group('pe', fns, reads=[wk, ('aT', g % 2, 0), ('aT', g % 2, 1)], writes=[pk])
                        tr.op('dve', lambda pt=pt, m=m, t0=t0, tn=tn: nc.vector.scalar_tensor_tensor(
                            out=xT[:, m, t0:t0 + tn], in0=pt[:, 0:tn], scalar=0.5, in1=xT[:, m, t0:t0 + tn],
                            op0=ALU.mult, op1=ALU.add), reads=[pk, ('xT', m)], writes=[('xT', m)])
                wrel(wn)

            for g in range(NG):
                wg, wgk, wgn = wnext(('gate', l, i, g))
                wu, wuk, wun = wnext(('up', l, i, g))
                a = aT[g % 2]
                for j in range(2):
                    for t, (t0, tn) in enumerate(TG):
                        pg, pgk = psA()
                        pu, puk = psA()
                        for (w_, wk_, p_, pk_) in ((wg, wgk, pg, pgk), (wu, wuk, pu, puk)):
                            fns = [(lambda c=c, w_=w_, p_=p_, t0=t0, tn=tn, j=j: nc.tensor.matmul(
                                p_[:, 0:tn], lhsT=w_[:, c, j * 128:(j + 1) * 128], rhs=hT[:, c, t0:t0 + tn],
                                start=(c == 0), stop=(c == NCH - 1))) for c in range(NCH)]
                            tr.group('pe', fns, reads=[wk_] + hkeys, writes=[pk_])
                        si = (j * 3 + t) % 3
                        s_ = stmp[si]
                        if flags.get('dbg') and g == 0 and j == 0 and t == 0 and l == 0 and i == 0:
                            dbt = sb('dbt', [128, 3 * 344], F32)
                            tr.op('dve', lambda: nc.vector.tensor_copy(out=dbt[:, 0:344], in_=pg[:, 0:344]), reads=[pgk], writes=['dbt'])
                            tr.op('dve', lambda: nc.vector.tensor_copy(out=dbt[:, 344:688], in_=pu[:, 0:344]), reads=[puk], writes=['dbt'])
                        tr.op('act', lambda s_=s_, pg=pg, tn=tn: nc.scalar.activation(
                            out=s_[:, 0:tn], in_=pg[:, 0:tn], func=AF.Silu), reads=[pgk], writes=[('stmp', si)])
                        tr.op('dve', lambda s_=s_, pu=pu, a=a, j=j, t0=t0, tn=tn: nc.vector.tensor_tensor(
                            out=a[:, j, t0:t0 + tn], in0=s_[:, 0:tn], in1=pu[:, 0:tn], op=ALU.mult),
                            reads=[puk, ('stmp', si)], writes=[('aT', g % 2, j)])
                        if flags.get('dbg') and g == 0 and j == 0 and t == 0 and l == 0 and i == 0:
                            tr.op('dve', lambda: nc.vector.tensor_copy(out=dbt[:, 688:1032], in_=s_[:, 0:344]), reads=[('stmp', si)], writes=['dbt'])
                            tr.dma('sp', dr['dbg_p'][:, :], dbt[:, :], reads=['dbt'], is_out=True)
                wrel(wgn)
                wrel(wun)
                if flags.get('dbg') and g == 0 and l == 0 and i == 0:
                    tr.dma('sp', dr['dbg_h'][:, :], hT[:].rearrange("p c t -> p (c t)"), reads=hkeys, is_out=True)
                    tr.dma('sp', dr['dbg_a'][:, :], aT[0][:].rearrange("p c t -> p (c t)"), reads=[('aT', 0, 0), ('aT', 0, 1)], is_out=True)
                if pend is not None:
                    down(pend)
                pend = g
                if flags.get('dbg') and g == 1 and l == 0 and i == 0:
                    tr.dma('sp', dr['dbg_x'][:, :], xT[:].rearrange("p c t -> p (c t)"), reads=xkeys, is_out=True)
            down(pend)

        def final_out(q):
            alloc_io()
            compute_rstd(float(D))
            for c in range(NCH):
                tr.op('dve', lambda c=c: nc.vector.scalar_tensor_tensor(
                    out=xT[:, c, :], in0=xT[:, c, :], scalar=nw[:, 12 * NCH + c:12 * NCH + c + 1],
                    in1=rstd[:, :], op0=ALU.mult, op1=ALU.mult),
                    reads=[('xT', c), 'nw'] + rkeys, writes=[('xT', c)])
            for i in range(9):
                buf = io[i % 2]
                bkey = ('io', i % 2)
                n = 128 if i < 8 else TS
                for cg in range(4):
                    pt, pk = psA()
                    for cc in range(4):
                        c = cg * 4 + cc
                        tr.op('pe', lambda c=c, cc=cc, pt=pt, n=n, i=i: nc.tensor.transpose(
                            out=pt[0:n, cc * 128:(cc + 1) * 128], in_=xT[:, c, i * 128:i * 128 + n],
                            identity=ident[:, :]), reads=[('xT', c), 'ident'], writes=[pk])
                    tr.op('dve', lambda pt=pt, n=n, buf=buf, cg=cg: nc.vector.tensor_copy(
                        out=buf[0:n, cg * 512:(cg + 1) * 512], in_=pt[0:n, :]), reads=[pk], writes=[bkey])
                if i < 8:
                    tr.dma('sp', dr['yp'][q * TP + i * 128:q * TP + (i + 1) * 128, :], buf[:, :], reads=[bkey], is_out=True)
                elif q == 0:
                    tr.dma('sp', dr['ys'][:, :], buf[0:TS, :], reads=[bkey], is_out=True)

        def load_xd(q):
            tr.dma('sp', xT[:, :, 0:TP], Xd[q].rearrange("p (c t) -> p c t", c=NCH), reads=[('Xd', q)], writes=xkeys)
            tr.dma('sp', xT[:, :, TP:T], Xs[:, :].rearrange("p (c t) -> p c t", c=NCH), reads=['Xs'], writes=xkeys)

        def store_xd(q):
            tr.dma('sp', Xd[q].rearrange("p (c t) -> p c t", c=NCH), xT[:, :, 0:TP], reads=xkeys, writes=[('Xd', q)])
            if q == 0:
                tr.dma('sp', Xs[:, :].rearrange("p (c t) -> p c t", c=NCH), xT[:, :, TP:T], reads=xkeys, writes=['Xs'])


        PROMPT_CHUNKS = [(j // 3, 32 * (j % 3), 32, 32 * j, j) for j in range(32)]
        SAMPLE_CHUNK = [(NTT - 1, 0, TS, TP, 32)]

        def glr_state_only(R):
            KC, dv = R['KC'], R['dv']
            for (tile, pb, cs, col0, j) in PROMPT_CHUNKS:
                for kc in range(KC):
                    p_, pk = psA()
                    tr.group('pe', [lambda p_=p_, kc=kc, pb=pb, cs=cs, tile=tile: nc.tensor.matmul(
                        p_[:, 0:dv], lhsT=R['khat'][pb:pb + cs, tile, kc * 128:(kc + 1) * 128],
                        rhs=R['vtm'][pb:pb + cs, tile, 0:dv], start=True, stop=True)],
                        reads=['khat', 'vtm'], writes=[pk])
                    tr.op('dve', lambda p_=p_, kc=kc, j=j: nc.vector.scalar_tensor_tensor(
                        out=R['S'][:, kc, :], in0=R['S'][:, kc, :], scalar=R['dT'][:, kc, j:j + 1],
                        in1=p_[:, 0:dv], op0=ALU.mult, op1=ALU.add),
                        reads=[pk, ('S', kc), 'dT'], writes=[('S', kc)])

        def glr_full(R, chunks):
            KC, dv = R['KC'], R['dv']
            VC = dv // 128
            for (tile, pb, cs, col0, j) in chunks:
                pa, pak = psA()
                tr.group('pe', [lambda kc=kc, pa=pa, pb=pb, cs=cs, col0=col0: nc.tensor.matmul(
                    pa[pb:pb + cs, 0:cs], lhsT=R['kT'][:, kc, col0:col0 + cs], rhs=R['qT'][:, kc, col0:col0 + cs],
                    start=(kc == 0), stop=(kc == KC - 1)) for kc in range(KC)],
                    reads=['kT', 'qT'], writes=[pak])
                tr.op('dve', lambda pa=pa, pb=pb, cs=cs: nc.vector.tensor_tensor(
                    out=R['A'][pb:pb + cs, 0:cs], in0=pa[pb:pb + cs, 0:cs], in1=cmask[pb:pb + cs, 0:cs],
                    op=ALU.mult), reads=[pak, 'cmask'], writes=['A'])
                po, pok = psA()
                fns = []
                for vc in range(VC):
                    fns.append(lambda vc=vc, po=po, pb=pb, cs=cs, tile=tile: nc.tensor.matmul(
                        po[:, vc * 32:vc * 32 + cs], lhsT=R['vtm'][pb:pb + cs, tile, vc * 128:(vc + 1) * 128],
                        rhs=R['A'][pb:pb + cs, 0:cs], start=True, stop=False))
                    for kc in range(KC):
                        fns.append(lambda vc=vc, kc=kc, po=po, cs=cs, col0=col0: nc.tensor.matmul(
                            po[:, vc * 32:vc * 32 + cs], lhsT=R['Sbf'][:, kc, vc * 128:(vc + 1) * 128],
                            rhs=R['qT'][:, kc, col0:col0 + cs], start=False, stop=(kc == KC - 1)))
                tr.group('pe', fns, reads=['vtm', 'A', 'qT'] + [('Sbf', kc) for kc in range(KC)], writes=[pok])
                tr.op('act', lambda po=po, cs=cs, col0=col0: nc.scalar.copy(
                    out=R['oT'][:, :, col0:col0 + cs],
                    in_=po[:, 0:VC * 32].rearrange("p (v s) -> p v s", s=32)[:, :, 0:cs]),
                    reads=[pok], writes=['oT'])
                for kc in range(KC):
                    p_, pk = psA()
                    tr.group('pe', [lambda p_=p_, kc=kc, pb=pb, cs=cs, tile=tile: nc.tensor.matmul(
                        p_[:, 0:dv], lhsT=R['khat'][pb:pb + cs, tile, kc * 128:(kc + 1) * 128],
                        rhs=R['vtm'][pb:pb + cs, tile, 0:dv], start=True, stop=True)],
                        reads=['khat', 'vtm'], writes=[pk])
                    tr.op('dve', lambda p_=p_, kc=kc, j=j: nc.vector.scalar_tensor_tensor(
                        out=R['S'][:, kc, :], in0=R['S'][:, kc, :], scalar=R['dT'][:, kc, j:j + 1],
                        in1=p_[:, 0:dv], op0=ALU.mult, op1=ALU.add),
                        reads=[pk, ('S', kc), 'dT'], writes=[('S', kc)])
                    tr.op('act', lambda kc=kc: nc.scalar.copy(out=R['Sbf'][:, kc, :], in_=R['S'][:, kc, :]),
                          reads=[('S', kc)], writes=[('Sbf', kc)])

        def glr_head(R, q, pkey, s_in_ap, p_ap, s_out_ap):
            KC = R['KC']
            Skeys = [('S', kc) for kc in range(KC)]

            def tobf():
                for kc in range(KC):
                    tr.op('act', lambda kc=kc: nc.scalar.copy(out=R['Sbf'][:, kc, :], in_=R['S'][:, kc, :]),
                          reads=[('S', kc)], writes=[('Sbf', kc)])
            if q == 0:
                tr.op('pool', lambda: nc.gpsimd.memset(R['S'][:], 0.0), reads=Skeys, writes=Skeys)
            else:
                tr.dma('sp', R['S'][:], p_ap, reads=[pkey], writes=Skeys)
            tobf()
            glr_full(R, PROMPT_CHUNKS)
            tr.dma('sp', p_ap, R['S'][:], reads=Skeys, writes=[pkey], is_out=True)
            if q == 0:
                tr.dma('sp', R['S'][:], s_in_ap, writes=Skeys)
                tobf()
                glr_full(R, SAMPLE_CHUNK)
                tr.dma('sp', s_out_ap, R['S'][:], reads=Skeys, is_out=True)
            else:
                tr.op('pool', lambda: nc.gpsimd.memset(R['oT'][:, :, TP:T], 0.0), reads=['oT'], writes=['oT'])

        def proj_fm(w_, wk_, col, evac):
            for t, (t0, tn) in enumerate(TG):
                pt, pk = psA()
                tr.group('pe', [lambda c=c, pt=pt, t0=t0, tn=tn: nc.tensor.matmul(
                    pt[:, 0:tn], lhsT=w_[:, c, col:col + 128], rhs=hT[:, c, t0:t0 + tn],
                    start=(c == 0), stop=(c == NCH - 1)) for c in range(NCH)], reads=[wk_] + hkeys, writes=[pk])
                evac(pt, t0, tn, pk)

        def proj_tm(w_, wk_, i, ncols, evac):
            c0, n = TT[i]
            pt, pk = psA()
            tr.group('pe', [lambda c=c, pt=pt: nc.tensor.matmul(
                pt[0:n, 0:ncols], lhsT=hT[:, c, c0:c0 + n], rhs=w_[:, c, 0:ncols],
                start=(c == 0), stop=(c == NCH - 1)) for c in range(NCH)], reads=[wk_] + hkeys, writes=[pk])
            evac(pt, n, pk)

        def outproj_rows(desc, oT_, okey, vc0, scale=1.0):
            wo, wok, won = wnext(desc)
            for m in range(NCH):
                for t, (t0, tn) in enumerate(TG):
                    pt, pk = psB()
                    tr.group('pe', [lambda jj=jj, pt=pt, m=m, t0=t0, tn=tn: nc.tensor.matmul(
                        pt[:, 0:tn], lhsT=wo[:, jj, m * 128:(m + 1) * 128], rhs=oT_[:, vc0 + jj, t0:t0 + tn],
                        start=(jj == 0), stop=(jj == 1)) for jj in range(2)], reads=[wok, okey], writes=[pk])
                    tr.op('dve', lambda pt=pt, m=m, t0=t0, tn=tn: nc.vector.scalar_tensor_tensor(
                        out=xT[:, m, t0:t0 + tn], in0=pt[:, 0:tn], scalar=scale, in1=xT[:, m, t0:t0 + tn],
                        op0=ALU.mult, op1=ALU.add), reads=[pk, ('xT', m)], writes=[('xT', m)])
            wrel(won)

        def gla_layer(l, nidx, q):
            o = l // 2
            AR.reset()
            rmsnorm(nidx)
            wa = AR([16, 16], BF16)
            wupb = AR([1024], BF16)
            aT17 = AR([T], BF16)
            R = {'KC': 2, 'dv': 512}
            qk = AR([4, T], BF16)
            R['qT'] = qk[:, 0:2, :]
            R['kT'] = qk[:, 2:4, :]
            R['khat'] = AR([NTT, 256], BF16)
            R['vtm'] = AR([NTT, 512], BF16)
            R['dT'] = AR([2, 33], F32)
            R['S'] = AR([2, 512], F32)
            R['Sbf'] = AR([2, 512], BF16)
            R['oT'] = AR([4, T], BF16)
            X1 = AR([T], F32)
            X2 = AR([T], F32)
            X3 = AR([T], F32)
            tmpA = AR([256], F32)
            tmpB = AR([256], F32)
            R['A'] = AR([32], BF16)
            st2 = [AR([344], F32) for _ in range(2)]
            tr.dma('pool', wa[:], dr['gla_w_in'][o][:, 6144:6160].rearrange("(c p) f -> p c f", p=128), writes=['wa'])
            tr.dma('pool', wupb[0:16, :], dr['gla_w_gate_up'][o][:, :], writes=['wupb'])
            tr.dma('pool', wupb[16:17, :], dr['gla_b_gate'][o][:, :], writes=['wupb'])
            tr.op('pool', lambda: nc.gpsimd.memset(aT17[0:32, :], 1.0), writes=['aT17'])
            for t, (t0, tn) in enumerate(TG):
                pt, pk = psA()
                tr.group('pe', [lambda c=c, pt=pt, t0=t0, tn=tn: nc.tensor.matmul(
                    pt[0:16, 0:tn], lhsT=wa[:, c, 0:16], rhs=hT[:, c, t0:t0 + tn],
                    start=(c == 0), stop=(c == NCH - 1)) for c in range(NCH)], reads=['wa'] + hkeys, writes=[pk])
                tr.op('dve', lambda pt=pt, t0=t0, tn=tn: nc.vector.tensor_copy(
                    out=aT17[0:16, t0:t0 + tn], in_=pt[0:16, 0:tn]), reads=[pk, 'aT17'], writes=['aT17'])

            for h in range(4):
                wq, wqk, wqn = wnext(('glain', o, h))
                wkt, wkk, wkn = wnext(('glain', o, 4 + h))
                for kc in range(2):
                    ch0 = h * 256 + kc * 128
                    for t, (t0, tn) in enumerate(TG):
                        pt, pk = psA()
                        tr.group('pe', [lambda pt=pt, t0=t0, tn=tn, ch0=ch0: nc.tensor.matmul(
                            pt[:, 0:tn], lhsT=wupb[0:17, ch0:ch0 + 128], rhs=aT17[0:17, t0:t0 + tn],
                            start=True, stop=True)], reads=['wupb', 'aT17'], writes=[pk])
                        tr.op('act', lambda pt=pt, t0=t0, tn=tn: nc.scalar.activation(
                            out=X1[:, t0:t0 + tn], in_=pt[:, 0:tn], func=AF.Exp, scale=-1.0), reads=[pk], writes=['X1'])
                    tr.op('act', lambda: nc.scalar.activation(out=X1[:, :], in_=X1[:, :], func=AF.Ln, bias=onec[:, 0:1]),
                          reads=['X1', 'onec'], writes=['X1'])
                    tr.op('dve', lambda: nc.vector.tensor_tensor_scan(
                        out=X2[:, :], data0=rmask[:, :], data1=X1[:, :], initial=0.0, op0=ALU.mult, op1=ALU.add),
                        reads=['X1', 'rmask'], writes=['X2'])
                    tr.op('act', lambda: nc.scalar.activation(out=X1[:, :], in_=X2[:, :], func=AF.Exp, scale=-1.0 / 16),
                          reads=['X2'], writes=['X1'])
                    tr.op('act', lambda: nc.scalar.activation(out=X3[:, :], in_=X2[:, :], func=AF.Exp, scale=1.0 / 16),
                          reads=['X2'], writes=['X3'])
                    tr.op('dve', lambda kc=kc: nc.vector.tensor_copy(
                        out=R['dT'][:, kc, 0:32], in_=X1[:, 0:1024].rearrange("p (j s) -> p j s", s=32)[:, :, 31]),
                        reads=['X1'], writes=['dT'])
                    tr.op('dve', lambda kc=kc: nc.vector.tensor_copy(out=R['dT'][:, kc, 32:33], in_=X1[:, T - 1:T]),
                          reads=['X1'], writes=['dT'])
                    proj_fm(wq, wqk, kc * 128, lambda pt, t0, tn, pk, kc=kc: tr.op(
                        'dve', lambda: nc.vector.scalar_tensor_tensor(
                            out=R['qT'][:, kc, t0:t0 + tn], in0=pt[:, 0:tn], scalar=1.0 / 16, in1=X1[:, t0:t0 + tn],
                            op0=ALU.mult, op1=ALU.mult), reads=[pk, 'X1'], writes=['qT']))
                    proj_fm(wkt, wkk, kc * 128, lambda pt, t0, tn, pk, kc=kc: tr.op(
                        'dve', lambda: nc.vector.tensor_tensor(
                            out=R['kT'][:, kc, t0:t0 + tn], in0=pt[:, 0:tn], in1=X3[:, t0:t0 + tn], op=ALU.mult),
                        reads=[pk, 'X3'], writes=['kT']))
                wrel(wqn)
                for i in range(NTT):
                    c0, n = TT[i]
                    pz, pzk = psA()
                    tr.group('pe', [lambda pz=pz, n=n, c0=c0: nc.tensor.matmul(
                        pz[0:n, 0:256], lhsT=aT17[0:17, c0:c0 + n], rhs=wupb[0:17, h * 256:(h + 1) * 256],
                        start=True, stop=True)], reads=['wupb', 'aT17'], writes=[pzk])
                    tr.op('act', lambda pz=pz, n=n: nc.scalar.activation(
                        out=tmpA[0:n, :], in_=pz[0:n, 0:256], func=AF.Exp, scale=-1.0), reads=[pzk], writes=['tmpA'])
                    tr.op('act', lambda n=n: nc.scalar.activation(
                        out=tmpA[0:n, :], in_=tmpA[0:n, :], func=AF.Ln, bias=onec[0:n, 0:1]),
                        reads=['tmpA', 'onec'], writes=['tmpA'])
                    pr, prk = psA()
                    tr.group('pe', [lambda pr=pr, n=n: nc.tensor.matmul(
                        pr[0:n, 0:256], lhsT=revtri[0:n, 0:n], rhs=tmpA[0:n, :], start=True, stop=True)],
                        reads=['tmpA', 'revtri'], writes=[prk])
                    tr.op('act', lambda pr=pr, n=n: nc.scalar.activation(
                        out=tmpB[0:n, :], in_=pr[0:n, 0:256], func=AF.Exp, scale=-1.0 / 16), reads=[prk], writes=['tmpB'])
                    proj_tm(wkt, wkk, i, 256, lambda pt, n, pk, i=i: tr.op(
                        'dve', lambda: nc.vector.tensor_tensor(
                            out=R['khat'][0:n, i, :], in0=pt[0:n, 0:256], in1=tmpB[0:n, :], op=ALU.mult),
                        reads=[pk, 'tmpB'], writes=['khat']))
                wrel(wkn)
                for half in range(2):
                    wv, wvk, wvn = wnext(('glain', o, 8 + 2 * h + half))
                    for i in range(NTT):
                        proj_tm(wv, wvk, i, 256, lambda pt, n, pk, i=i, half=half: tr.op(
                            'act', lambda: nc.scalar.copy(out=R['vtm'][0:n, i, half * 256:(half + 1) * 256],
                                                          in_=pt[0:n, 0:256]), reads=[pk], writes=['vtm']))
                    wrel(wvn)
                glr_head(R, q, ('gla_p', o, h),
                         dr['sgla'][o, h].rearrange("(k p) v -> p k v", p=128),
                         dr['gla_p'][o, h].rearrange("(k p) v -> p k v", p=128),
                         dr['gla_s'][o, h].rearrange("(k p) v -> p k v", p=128))
                sq = qk
                for vc in range(4):
                    tr.op('act', lambda vc=vc: nc.scalar.activation(out=sq[:, vc, :], in_=R['oT'][:, vc, :], func=AF.Square),
                          reads=['oT', 'qT', 'kT'], writes=['sq'])
                for t, (t0, tn) in enumerate(TG):
                    pt, pk = psB()
                    tr.group('pe', [lambda vc=vc, pt=pt, t0=t0, tn=tn: nc.tensor.matmul(
                        pt[:, 0:tn], lhsT=ones_bf[:, :], rhs=sq[:, vc, t0:t0 + tn], start=(vc == 0), stop=(vc == 3))
                        for vc in range(4)], reads=['sq', 'ones'], writes=[pk])
                    tr.op('act', lambda pt=pt, t0=t0, tn=tn: nc.scalar.activation(
                        out=X2[:, t0:t0 + tn], in_=pt[:, 0:tn], func=AF.Sqrt, bias=epsc[:, 0:1], scale=1.0 / 512),
                        reads=[pk, 'epsc'], writes=['X2'])
                tr.op('dve', lambda: nc.vector.reciprocal(out=X2[:, :], in_=X2[:, :]), reads=['X2'], writes=['X2'])
                for half in range(2):
                    wr, wrk, wrn = wnext(('glain', o, 16 + 2 * h + half))
                    for jj in range(2):
                        vc = half * 2 + jj

                        def evac(pt, t0, tn, pk, vc=vc):
                            si = vc % 2
                            tr.op('act', lambda: nc.scalar.activation(out=st2[si][:, 0:tn], in_=pt[:, 0:tn], func=AF.Silu),
                                  reads=[pk], writes=[('st2', si)])
                            tr.op('dve', lambda: nc.vector.scalar_tensor_tensor(
                                out=st2[si][:, 0:tn], in0=st2[si][:, 0:tn], scalar=gnw[:, o * 4 + vc:o * 4 + vc + 1],
                                in1=X2[:, t0:t0 + tn], op0=ALU.mult, op1=ALU.mult),
                                reads=[('st2', si), 'gnw', 'X2'], writes=[('st2', si)])
                            tr.op('dve', lambda: nc.vector.tensor_tensor(
                                out=R['oT'][:, vc, t0:t0 + tn], in0=R['oT'][:, vc, t0:t0 + tn], in1=st2[si][:, 0:tn],
                                op=ALU.mult), reads=[('st2', si), 'oT'], writes=['oT'])
                        proj_fm(wr, wrk, jj * 128, evac)
                    wrel(wrn)
                for rt in range(2):
                    outproj_rows(('glaout', o, 2 * h + rt), R['oT'], 'oT', rt * 2)


        BIGNEG = -30000.0
        SCALE = 128.0 ** -0.5

        def even_layer(l, nidx, q):
            e = l // 2
            has_lb = (e == 1)
            AR.reset()
            rmsnorm(nidx)
            XX = AR([2 * T], F32)
            X1 = XX[:, 0:T]
            X2 = XX[:, T:2 * T]
            X3 = AR([T], F32)
            lbrow = AR([1024], F32)
            omlrow = AR([1024], F32)
            lbT = AR([8], F32)
            omlT = AR([8], F32)
            st2 = [AR([344], F32) for _ in range(2)]
            if has_lb:
                tr.dma('sp', XX[:, 0:2048], dr['lbrow'][:, :], writes=['X1', 'X2'])
                tr.op('dve', lambda: nc.vector.tensor_tensor(out=lbrow[:, :], in0=XX[:, 1024:2048], in1=XX[:, 0:1024],
                                                             op=ALU.subtract), reads=['X1', 'X2'], writes=['lbrow'])
                tr.op('act', lambda: nc.scalar.activation(out=lbrow[:, :], in_=lbrow[:, :], func=AF.Sigmoid),
                      reads=['lbrow'], writes=['lbrow'])
                tr.op('dve', lambda: nc.vector.tensor_scalar(out=omlrow[:, :], in0=lbrow[:, :], scalar1=-1.0, scalar2=1.0,
                                                             op0=ALU.mult, op1=ALU.add), reads=['lbrow'], writes=['omlrow'])
                tr.dma('sp', XX[:, 0:16], dr['lbT'][:, :], reads=['X1', 'X2'], writes=['X1', 'X2'])
                tr.op('dve', lambda: nc.vector.tensor_tensor(out=lbT[:, :], in0=XX[:, 8:16], in1=XX[:, 0:8],
                                                             op=ALU.subtract), reads=['X1', 'X2'], writes=['lbT'])
                tr.op('act', lambda: nc.scalar.activation(out=lbT[:, :], in_=lbT[:, :], func=AF.Sigmoid),
                      reads=['lbT'], writes=['lbT'])
                tr.op('dve', lambda: nc.vector.tensor_scalar(out=omlT[:, :], in0=lbT[:, :], scalar1=-1.0, scalar2=1.0,
                                                             op0=ALU.mult, op1=ALU.add), reads=['lbT'], writes=['omlT'])
            mark = AR.off

            for hp in range(4 if flags.get('ev_hgrn', True) else 0):
                AR.off = mark
                khat2 = AR([NTT, 256], BF16)
                vtm2 = AR([NTT, 256], BF16)
                oTp = AR([2, T], BF16)
                tmpA = AR([256], F32)
                tmpB = AR([256], F32)
                tmpL = AR([256], F32)
                sqb = AR([T], BF16)
                Rs = []
                for hh in range(2):
                    R = {'KC': 1, 'dv': 128}
                    R['qT'] = AR([1, T], BF16)
                    R['kT'] = AR([1, T], BF16)
                    R['oT'] = AR([1, T], BF16)
                    R['dT'] = AR([1, 33], F32)
                    R['S'] = AR([1, 128], F32)
                    R['Sbf'] = AR([1, 128], BF16)
                    R['A'] = AR([32], BF16)
                    R['khat'] = khat2[:, :, hh * 128:(hh + 1) * 128]
                    R['vtm'] = vtm2[:, :, hh * 128:(hh + 1) * 128]
                    Rs.append(R)
                wq, wqk, wqn = wnext(('evin', e, 0 + hp))
                wf, wfk, wfn = wnext(('evin', e, 4 + hp))
                for hh in range(2):
                    R = Rs[hh]
                    ch = hp * 2 + hh
                    proj_fm(wf, wfk, hh * 128, lambda pt, t0, tn, pk: tr.op('act', lambda: nc.scalar.activation(
                        out=X1[:, t0:t0 + tn], in_=pt[:, 0:tn], func=AF.Sigmoid), reads=[pk], writes=['X1']))
                    if has_lb:
                        tr.op('dve', lambda ch=ch: nc.vector.tensor_scalar(
                            out=X1[:, :], in0=X1[:, :], scalar1=omlT[:, ch:ch + 1], scalar2=lbT[:, ch:ch + 1],
                            op0=ALU.mult, op1=ALU.add), reads=['X1', 'lbT', 'omlT'], writes=['X1'])
                    tr.op('act', lambda: nc.scalar.activation(out=X2[:, :], in_=X1[:, :], func=AF.Ln),
                          reads=['X1'], writes=['X2'])
                    tr.op('dve', lambda: nc.vector.tensor_tensor_scan(
                        out=X3[:, :], data0=rmask[:, :], data1=X2[:, :], initial=0.0, op0=ALU.mult, op1=ALU.add),
                        reads=['X2', 'rmask'], writes=['X3'])
                    tr.op('dve', lambda: nc.vector.tensor_scalar(out=X1[:, :], in0=X1[:, :], scalar1=-1.0, scalar2=1.0,
                                                                 op0=ALU.mult, op1=ALU.add), reads=['X1'], writes=['X1'])
                    tr.op('act', lambda: nc.scalar.activation(out=X2[:, :], in_=X3[:, :], func=AF.Exp, scale=-1.0),
                          reads=['X3'], writes=['X2'])
                    tr.op('dve', lambda R=R: nc.vector.tensor_tensor(out=R['kT'][:, 0, :], in0=X1[:, :], in1=X2[:, :],
                                                                     op=ALU.mult), reads=['X1', 'X2'], writes=['kT'])
                    tr.op('act', lambda: nc.scalar.activation(out=X2[:, :], in_=X3[:, :], func=AF.Exp),
                          reads=['X3', 'kT'], writes=['X2'])
                    tr.op('dve', lambda R=R: nc.vector.tensor_copy(
                        out=R['dT'][:, 0, 0:32], in_=X2[:, 0:1024].rearrange("p (j s) -> p j s", s=32)[:, :, 31]),
                        reads=['X2'], writes=['dT'])
                    tr.op('dve', lambda R=R: nc.vector.tensor_copy(out=R['dT'][:, 0, 32:33], in_=X2[:, T - 1:T]),
                          reads=['X2'], writes=['dT'])

                    def evq(pt, t0, tn, pk, R=R):
                        si = (t0 // 344) % 2
                        tr.op('act', lambda: nc.scalar.activation(out=st2[si][:, 0:tn], in_=pt[:, 0:tn], func=AF.Silu),
                              reads=[pk], writes=[('st2', si)])
                        tr.op('dve', lambda: nc.vector.scalar_tensor_tensor(
                            out=R['qT'][:, 0, t0:t0 + tn], in0=st2[si][:, 0:tn], scalar=SCALE, in1=X2[:, t0:t0 + tn],
                            op0=ALU.mult, op1=ALU.mult), reads=[('st2', si), 'X2'], writes=['qT'])
                    proj_fm(wq, wqk, hh * 128, evq)
                wrel(wqn)
                for i in range(NTT):
                    c0, n = TT[i]
                    proj_tm(wf, wfk, i, 256, lambda pt, n, pk: tr.op('act', lambda: nc.scalar.activation(
                        out=tmpA[0:n, :], in_=pt[0:n, 0:256], func=AF.Sigmoid), reads=[pk], writes=['tmpA']))
                    if has_lb:
                        tr.op('dve', lambda n=n: nc.vector.tensor_tensor(
                            out=tmpA[0:n, :], in0=tmpA[0:n, :], in1=omlrow[0:n, hp * 256:(hp + 1) * 256], op=ALU.mult),
                            reads=['tmpA', 'omlrow'], writes=['tmpA'])
                        tr.op('dve', lambda n=n: nc.vector.tensor_tensor(
                            out=tmpA[0:n, :], in0=tmpA[0:n, :], in1=lbrow[0:n, hp * 256:(hp + 1) * 256], op=ALU.add),
                            reads=['tmpA', 'lbrow'], writes=['tmpA'])
                    tr.op('act', lambda n=n: nc.scalar.activation(out=tmpL[0:n, :], in_=tmpA[0:n, :], func=AF.Ln),
                          reads=['tmpA'], writes=['tmpL'])
                    pr, prk = psA()
                    tr.group('pe', [lambda pr=pr, n=n: nc.tensor.matmul(
                        pr[0:n, 0:256], lhsT=revtri[0:n, 0:n], rhs=tmpL[0:n, :], start=True, stop=True)],
                        reads=['tmpL', 'revtri'], writes=[prk])
                    tr.op('act', lambda pr=pr, n=n: nc.scalar.activation(out=tmpB[0:n, :], in_=pr[0:n, 0:256], func=AF.Exp),
                          reads=[prk], writes=['tmpB'])
                    tr.op('dve', lambda n=n: nc.vector.tensor_scalar(out=tmpA[0:n, :], in0=tmpA[0:n, :], scalar1=-1.0,
                                                                     scalar2=1.0, op0=ALU.mult, op1=ALU.add),
                          reads=['tmpA'], writes=['tmpA'])
                    tr.op('dve', lambda n=n, i=i: nc.vector.tensor_tensor(out=khat2[0:n, i, :], in0=tmpA[0:n, :],
                                                                          in1=tmpB[0:n, :], op=ALU.mult),
                          reads=['tmpA', 'tmpB'], writes=['khat'])
                wrel(wfn)
                wi, wik, win = wnext(('evin', e, 8 + hp))
                for i in range(NTT):
                    proj_tm(wi, wik, i, 256, lambda pt, n, pk, i=i: tr.op('act', lambda: nc.scalar.copy(
                        out=vtm2[0:n, i, :], in_=pt[0:n, 0:256]), reads=[pk], writes=['vtm']))
                wrel(win)
                for hh in range(2):
                    hd = hp * 2 + hh
                    glr_head(Rs[hh], q, ('hgrn_p', e, hd),
                             dr['shgrn'][e, hd].rearrange("(k p) v -> p k v", p=128),
                             dr['hgrn_p'][e, hd].rearrange("(k p) v -> p k v", p=128),
                             dr['hgrn_s'][e, hd].rearrange("(k p) v -> p k v", p=128))
                wg, wgk, wgn = wnext(('evin', e, 12 + hp))
                for hh in range(2):
                    R = Rs[hh]
                    tr.op('act', lambda R=R: nc.scalar.activation(out=sqb[:, :], in_=R['oT'][:, 0, :], func=AF.Square),
                          reads=['oT'], writes=['sqb'])
                    for t, (t0, tn) in enumerate(TG):
                        pt, pk = psB()
                        tr.group('pe', [lambda pt=pt, t0=t0, tn=tn: nc.tensor.matmul(
                            pt[:, 0:tn], lhsT=ones_bf[:, :], rhs=sqb[:, t0:t0 + tn], start=True, stop=True)],
                            reads=['sqb', 'ones'], writes=[pk])
                        tr.op('act', lambda pt=pt, t0=t0, tn=tn: nc.scalar.activation(
                            out=X2[:, t0:t0 + tn], in_=pt[:, 0:tn], func=AF.Sqrt, bias=epsc[:, 0:1], scale=1.0 / 128),
                            reads=[pk, 'epsc'], writes=['X2'])
                    tr.op('dve', lambda: nc.vector.reciprocal(out=X2[:, :], in_=X2[:, :]), reads=['X2'], writes=['X2'])

                    def evg(pt, t0, tn, pk, R=R, hh=hh):
                        si = (t0 // 344) % 2
                        tr.op('act', lambda: nc.scalar.activation(out=st2[si][:, 0:tn], in_=pt[:, 0:tn], func=AF.Silu),
                              reads=[pk], writes=[('st2', si)])
                        tr.op('dve', lambda: nc.vector.scalar_tensor_tensor(
                            out=st2[si][:, 0:tn], in0=st2[si][:, 0:tn], scalar=hnw[:, e:e + 1], in1=X2[:, t0:t0 + tn],
                            op0=ALU.mult, op1=ALU.mult), reads=[('st2', si), 'hnw', 'X2'], writes=[('st2', si)])
                        tr.op('dve', lambda: nc.vector.tensor_tensor(
                            out=oTp[:, hh, t0:t0 + tn], in0=R['oT'][:, 0, t0:t0 + tn], in1=st2[si][:, 0:tn], op=ALU.mult),
                            reads=[('st2', si), 'oT'], writes=['oTp'])
                    proj_fm(wg, wgk, hh * 128, evg)
                wrel(wgn)
                outproj_rows(('evout', e, hp), oTp, 'oTp', 0)

            AR.reset()
            oTm = AR([8, T], BF16)
            qs32 = AR([8, 8], F32)
            ks32 = AR([8, 8], F32)
            vs32 = AR([1024], F32)
            smark = AR.off
            qTh = AR([2, T], BF16)
            kTh = AR([2, T], BF16)
            KTa = AR([4 * TP], BF16)
            Va = AR([32, 128], BF16)
            stg = [AR([256], F32) for _ in range(2)]
            PT = [AR([256], BF16) for _ in range(3)]
            biasT = AR([256], BF16)
            scp = AR([24], F32)
            mx8 = AR([8], F32)
            bq = AR([16], F32)
            kms = AR([4], F32)
            rden = AR([256], F32)
            tr.op('pool', lambda: nc.gpsimd.memset(biasT[:, :], 0.0), writes=['biasT'])
            tr.op('pool', lambda: nc.gpsimd.memset(oTm[:], 0.0), writes=['oTm'])
            for mp in range(4 if flags.get('ev_moba', True) else 0):
                wmq, wmqk, wmqn = wnext(('evin', e, 16 + mp))
                for hh in range(2):
                    def evmq(pt, t0, tn, pk, hh=hh):
                        tr.op('act', lambda: nc.scalar.copy(out=qTh[:, hh, t0:t0 + tn], in_=pt[:, 0:tn]),
                              reads=[pk], writes=['qTh'])
                        if t0 == 688 and q == 0 and flags.get('mb_qs', True):
                            tr.op('dve', lambda: nc.vector.tensor_copy(out=qs32[:, mp * 2 + hh, :], in_=qTh[:, hh, TP:T]),
                                  reads=['qTh'], writes=['qs32'])
                    proj_fm(wmq, wmqk, hh * 128, evmq)
                wrel(wmqn)
                wmk, wmkk, wmkn = wnext(('evin', e, 20 + mp))
                for hh in range(2):
                    def evmk(pt, t0, tn, pk, hh=hh):
                        tr.op('act', lambda: nc.scalar.copy(out=kTh[:, hh, t0:t0 + tn], in_=pt[:, 0:tn]),
                              reads=[pk], writes=['kTh'])
                        if t0 == 688 and q == 0 and flags.get('mb_qs', True):
                            tr.op('dve', lambda: nc.vector.tensor_copy(out=ks32[:, mp * 2 + hh, :], in_=kTh[:, hh, TP:T]),
                                  reads=['kTh'], writes=['ks32'])
                    proj_fm(wmk, wmkk, hh * 128, evmk)

                def kv_out(w_, wk_, name, sname):
                    for i in range(NTT if flags.get('mb_kvout', True) else 0):
                        c0, n = TT[i]
                        si = i % 2

                        def ev(pt, n, pk, i=i, c0=c0, si=si):
                            tr.op('act', lambda: nc.scalar.copy(out=stg[si][0:n, :], in_=pt[0:n, 0:256]),
                                  reads=[pk], writes=[('stg', si)])
                            if i < NTT - 1:
                                tr.dma('sp', dr[name][e, q * TP + c0:q * TP + c0 + n, mp * 256:(mp + 1) * 256],
                                       stg[si][0:n, :], reads=[('stg', si)], writes=[(name, mp)], is_out=True)
                            elif q == 0:
                                tr.dma('sp', dr[sname][e, :, mp * 256:(mp + 1) * 256], stg[si][0:n, :],
                                       reads=[('stg', si)], is_out=True)
                                if name == 'vp':
                                    tr.op('dve', lambda: nc.vector.tensor_copy(
                                        out=vs32[0:TS, mp * 256:(mp + 1) * 256], in_=stg[si][0:TS, :]),
                                        reads=[('stg', si)], writes=['vs32'])
                        proj_tm(w_, wk_, i, 256, ev)
                kv_out(wmk, wmkk, 'kp', 'ks')
                wrel(wmkn)
                wmv, wmvk, wmvn = wnext(('evin', e, 24 + mp))
                kv_out(wmv, wmvk, 'vp', 'vs')
                wrel(wmvn)
                for hh in range(2 if flags.get('mb_ktd', True) else 0):
                    hd = mp * 2 + hh
                    tr.dma('sp', KTd[hd][:, q * TP:(q + 1) * TP], kTh[:, hh, 0:TP], reads=['kTh'], writes=[('KTd', hd)])
                    tr.op('dve', lambda hh=hh: nc.vector.tensor_reduce(
                        out=kms[:, :], in_=kTh[:, hh, 0:TP].rearrange("p (b s) -> p b s", s=256), axis=AX.X, op=ALU.add),
                        reads=['kTh'], writes=['kms'])
                    tr.op('dve', lambda hd=hd: nc.vector.tensor_scalar(
                        out=kmT[:, hd, 4 * q:4 * q + 4], in0=kms[:, :], scalar1=1.0 / 256, scalar2=None, op0=ALU.mult),
                        reads=['kms'], writes=['kmT'])
                nk = (q + 1) * TP
                for hh in range(2 if flags.get('moba_attn', True) else 0):
                    hd = mp * 2 + hh
                    tr.dma('sp', KTa[:, 0:nk], KTd[hd][:, 0:nk], reads=[('KTd', hd)], writes=['KTa'])
                    tr.dma('pool', Va[:, 0:nk // 128, :],
                           dr['vp'][e][0:nk, hd * 128:(hd + 1) * 128].rearrange("(t p) d -> p t d", p=128),
                           reads=[('vp', mp)], writes=['Va'])
                    for Bl in range(4):
                        B = 4 * q + Bl
                        qc0 = Bl * 256
                        use_sel = B >= 4
                        if use_sel:
                            for qt in range(2):
                                tr.group('pe', [lambda qt=qt: nc.tensor.matmul(
                                    ps[6][:, 0:16], lhsT=qTh[:, hh, qc0 + qt * 128:qc0 + (qt + 1) * 128],
                                    rhs=kmT[:, hd, 0:16], start=True, stop=True)], reads=['qTh', 'kmT'], writes=[('ps', 6)])
                                tr.op('dve', lambda: nc.vector.tensor_copy(out=scp[:, :], in_=negc[:, :]),
                                      reads=['negc'], writes=['scp'])
                                tr.op('dve', lambda: nc.vector.tensor_copy(out=scp[:, 0:B], in_=ps[6][:, 0:B]),
                                      reads=[('ps', 6), 'scp'], writes=['scp'])
                                tr.op('dve', lambda: nc.vector.max(out=mx8[:, :], in_=scp[:, :]), reads=['scp'], writes=['mx8'])
                                tr.op('dve', lambda: nc.vector.tensor_scalar(
                                    out=bq[:, :], in0=scp[:, 0:16], scalar1=mx8[:, 2:3], scalar2=1.0,
                                    op0=ALU.is_ge, op1=ALU.subtract), reads=['scp', 'mx8'], writes=['bq'])
                                tr.op('dve', lambda: nc.vector.tensor_scalar(
                                    out=bq[:, :], in0=bq[:, :], scalar1=-BIGNEG, scalar2=None, op0=ALU.mult),
                                    reads=['bq'], writes=['bq'])
                                tr.op('pe', lambda: nc.tensor.transpose(out=ps[7][0:16, 0:128], in_=bq[:, :],
                                                                        identity=ident[:, :]),
                                      reads=['bq', 'ident'], writes=[('ps', 7)])
                                tr.op('act', lambda qt=qt: nc.scalar.copy(out=biasT[0:16, qt * 128:(qt + 1) * 128],
                                                                         in_=ps[7][0:16, 0:128]),
                                      reads=[('ps', 7), 'biasT'], writes=['biasT'])
                        ktiles = []
                        for n_ in range(B):
                            ktiles.append((2 * n_, n_, None))
                            ktiles.append((2 * n_ + 1, n_, None))
                        ktiles.append((2 * B, None, mskA))
                        ktiles.append((2 * B + 1, None, mskB))
                        for ki, (kt, nb, msk) in enumerate(ktiles):
                            pS, pSk = psA()
                            fns = [lambda pS=pS, kt=kt: nc.tensor.matmul(
                                pS[:, 0:256], lhsT=KTa[:, kt * 128:(kt + 1) * 128], rhs=qTh[:, hh, qc0:qc0 + 256],
                                start=True, stop=not (use_sel and nb is not None))]
                            rd = ['KTa', 'qTh']
                            if use_sel and nb is not None:
                                fns.append(lambda pS=pS, nb=nb: nc.tensor.matmul(
                                    pS[:, 0:256], lhsT=esel[0:32, nb, :], rhs=biasT[0:32, :], start=False, stop=True))
                                rd += ['esel', 'biasT']
                            tr.group('pe', fns, reads=rd, writes=[pSk])
                            pt_ = PT[ki % 3]
                            pk_ = ('PT', ki % 3)
                            tr.op('act', lambda pS=pS, pt_=pt_: nc.scalar.activation(
                                out=pt_[:, :], in_=pS[:, 0:256], func=AF.Exp, scale=SCALE), reads=[pSk], writes=[pk_])
                            if msk is not None:
                                tr.op('dve', lambda pt_=pt_, msk=msk: nc.vector.tensor_tensor(
                                    out=pt_[:, :], in0=pt_[:, :], in1=msk[:, :], op=ALU.mult),
                                    reads=[pk_, 'mskA', 'mskB'], writes=[pk_])
                            first = (ki == 0)
                            last = (ki == len(ktiles) - 1)
                            tr.group('pe', [lambda pt_=pt_, kt=kt: nc.tensor.matmul(
                                ps[4][:, 0:256], lhsT=Va[:, kt, :], rhs=pt_[:, :], start=first, stop=last)],
                                reads=['Va', pk_], writes=[('ps', 4)])
                            tr.group('pe', [lambda pt_=pt_: nc.tensor.matmul(
                                ps[5][:, 0:256], lhsT=ones_bf[:, :], rhs=pt_[:, :], start=first, stop=last)],
                                reads=['ones', pk_], writes=[('ps', 5)])
                        tr.op('dve', lambda: nc.vector.reciprocal(out=rden[:, :], in_=ps[5][:, 0:256]),
                              reads=[('ps', 5)], writes=['rden'])
                        tr.op('dve', lambda hd=hd: nc.vector.tensor_tensor(
                            out=oTm[:, hd, qc0:qc0 + 256], in0=ps[4][:, 0:256], in1=rden[:, :], op=ALU.mult),
                            reads=[('ps', 4), 'rden'], writes=['oTm'])
            if q == 0 and flags.get('sample_moba', SAMPLE_MOBA_DEFAULT):
                moba_sample(e, oTm, qs32, ks32, vs32, smark)
            else:
                tr.op('pool', lambda: nc.gpsimd.memset(oTm[:, :, TP:T], 0.0), reads=['oTm'], writes=['oTm'])
            for mp in range(4 if flags.get('ev_moba', True) else 0):
                outproj_rows(('evout', e, 4 + mp), oTm, 'oTm', 2 * mp)

        def moba_sample(e, oTm, qs32, ks32, vs32, smark):
            tr.barrier()
            AR.off = smark
            ptab_f = AR([128], F32)
            ptab_i = ptab_f.bitcast(I32)
            ptf = AR([128], F32)
            idx_f = AR([128], F32)
            idx = idx_f.bitcast(I32)
            iot_f = AR([1], F32)
            iot_i = iot_f.bitcast(I32)
            iotf = AR([1], F32)
            Kp = [AR([1024], F32) for _ in range(2)]
            KTp = AR([4, 128], F32)
            ksum = AR([4, 128], F32)
            ksb = AR([4, 64], F32)
            Ls = AR([128, 32], F32)
            sc = AR([4, 64], F32)
            mx = AR([8], F32)
            biasS = AR([4, 64], F32)
            Dg = AR([64, 8], F32)
            Pown = AR([4, 8], F32)
            ones32 = AR([128], F32)
            rdn = AR([32], F32)
            ck = dr['cache_k'].rearrange("e r c -> (e r) c")
            cv = dr['cache_v'].rearrange("e r c -> (e r) c")
            tr.op('pool', lambda: nc.gpsimd.memset(ones32[:, :], 1.0), writes=['ones32'])
            tr.dma('sp', ptab_i[:, :], dr['ptab'][:, :], writes=['ptab'])
            tr.op('pool', lambda: nc.gpsimd.iota(out=iot_i[:, :], pattern=[[0, 1]], base=0, channel_multiplier=1),
                  writes=['iot'])
            tr.op('dve', lambda: nc.vector.tensor_copy(out=iotf[:, :], in_=iot_i[:, :]), reads=['iot'], writes=['iotf'])
            tr.op('dve', lambda: nc.vector.tensor_copy(out=ptf[:, :], in_=ptab_i[:, :]), reads=['ptab'], writes=['ptf'])
            tr.op('dve', lambda: nc.vector.tensor_scalar(out=ptf[:, :], in0=ptf[:, :], scalar1=128.0, scalar2=iotf[:, 0:1],
                                                         op0=ALU.mult, op1=ALU.add), reads=['ptf', 'iotf'], writes=['ptf'])
            tr.op('dve', lambda: nc.vector.tensor_scalar(out=ptf[:, :], in0=ptf[:, :], scalar1=float(e * 1280 * 128), scalar2=None,
                                                         op0=ALU.add), reads=['ptf'], writes=['ptf'])
            tr.op('dve', lambda: nc.vector.tensor_copy(out=idx[:, :], in_=ptf[:, :]), reads=['ptf'], writes=['idx'])

            if 'bc' not in K.__dict__:
                K.bc = nc.gpsimd.to_reg(2 * 1280 * 128 - 1)

            def gather(src2d, p):
                buf = Kp[p % 2]
                bk = ('Kp', p % 2)
                tr.dma('pool', None, None, reads=['idx'], writes=[bk], fn=lambda: nc.gpsimd.indirect_dma_start(
                    out=buf[:, :], out_offset=None, in_=src2d[:, :],
                    in_offset=bass.IndirectOffsetOnAxis(ap=idx[:, p:p + 1], axis=0),
                    bounds_check=K.bc, oob_is_err=False))
                return buf, bk

            for hh2 in range(2):
                for p in range(128):
                    buf, bk = gather(ck, p)
                    pT, pTk = psA()
                    for h4 in range(4):
                        hd = hh2 * 4 + h4
                        tr.op('pe', lambda h4=h4, hd=hd, pT=pT, buf=buf: nc.tensor.transpose(
                            out=pT[:, h4 * 128:(h4 + 1) * 128], in_=buf[:, hd * 128:(hd + 1) * 128], identity=ident[:, :]),
                            reads=[bk, 'ident'], writes=[pTk])
                    tr.op('act', lambda pT=pT: nc.scalar.copy(out=KTp[:, :, :],
                                                             in_=pT[:, :].rearrange("p (h k) -> p h k", h=4)),
                          reads=[pTk], writes=['KTp'])
                    tr.op('dve', lambda p=p: nc.vector.tensor_reduce(
                        out=ksum[:, :, p], in_=KTp[:, :, :], axis=AX.X, op=ALU.add),
                        reads=['KTp'], writes=['ksum'])
                    pL, pLk = psA()
                    for h4 in range(4):
                        hd = hh2 * 4 + h4
                        tr.op('pe', lambda h4=h4, hd=hd, pL=pL: nc.tensor.matmul(
                            pL[:, h4 * 8:(h4 + 1) * 8], lhsT=KTp[:, h4, :], rhs=qs32[:, hd, :], start=True, stop=True),
                            reads=['KTp', 'qs32'], writes=[pLk])
                    tr.op('dve', lambda pL=pL, p=p: nc.vector.tensor_copy(out=Ls[:, p, :], in_=pL[:, 0:32]),
                          reads=[pLk], writes=['Ls'])
                if flags.get('sm_stage', 4) < 2:
                    continue
                ks4 = ksum[:, :, :].rearrange("p h (n two) -> p h n two", two=2)
                tr.op('dve', lambda: nc.vector.tensor_tensor(out=ksb[:, :, :], in0=ks4[:, :, :, 0], in1=ks4[:, :, :, 1],
                                                             op=ALU.add), reads=['ksum'], writes=['ksb'])
                pS, pSk = psA()
                for h4 in range(4):
                    hd = hh2 * 4 + h4
                    tr.op('pe', lambda h4=h4, hd=hd, pS=pS: nc.tensor.matmul(
                        pS[0:TS, h4 * 64:(h4 + 1) * 64], lhsT=qs32[:, hd, :], rhs=ksb[:, h4, :], start=True, stop=True),
                        reads=['ksb', 'qs32'], writes=[pSk])
                tr.op('dve', lambda pS=pS: nc.vector.tensor_copy(out=sc[0:TS, :, :],
                                                                 in_=pS[0:TS, 0:256].rearrange("p (h n) -> p h n", h=4)),
                      reads=[pSk], writes=['sc'])
                for h4 in range(4):
                    tr.op('dve', lambda h4=h4: nc.vector.max(out=mx[0:TS, :], in_=sc[0:TS, h4, :]), reads=['sc'], writes=['mx'])
                    tr.op('dve', lambda h4=h4: nc.vector.tensor_scalar(
                        out=biasS[0:TS, h4, :], in0=sc[0:TS, h4, :], scalar1=mx[0:TS, 2:3], scalar2=1.0,
                        op0=ALU.is_ge, op1=ALU.subtract), reads=['sc', 'mx'], writes=['biasS'])
                tr.op('dve', lambda: nc.vector.tensor_scalar(out=biasS[0:TS, :, :], in0=biasS[0:TS, :, :], scalar1=-BIGNEG,
                                                             scalar2=None, op0=ALU.mult), reads=['biasS'], writes=['biasS'])
                Ls5 = Ls[:, :, :].rearrange("p (n two) (h q) -> p n two h q", two=2, h=4)
                for h4 in range(4):
                    for qi in range(TS):
                        tr.op('dve', lambda h4=h4, qi=qi: nc.vector.tensor_scalar(
                            out=Dg[0:TS, :, qi], in0=biasS[0:TS, h4, :], scalar1=ident[0:TS, qi:qi + 1], scalar2=None,
                            op0=ALU.mult), reads=['biasS', 'ident'], writes=['Dg'])
                    pB, pBk = psA()
                    for half in range(2):
                        tr.op('pe', lambda pB=pB, half=half: nc.tensor.matmul(
                            pB[:, half * 256:(half + 1) * 256], lhsT=ones32[0:TS, 0:128],
                            rhs=Dg[0:TS, half * 32:(half + 1) * 32, :].rearrange("p n q -> p (n q)"),
                            start=True, stop=True), reads=['Dg', 'ones32'], writes=[pBk])
                    for pp in range(2):
                        tr.op('dve', lambda h4=h4, pp=pp, pB=pB: nc.vector.tensor_tensor(
                            out=Ls5[:, :, pp, h4, :], in0=Ls5[:, :, pp, h4, :],
                            in1=pB[:, 0:512].rearrange("p (n q) -> p n q", q=TS), op=ALU.add),
                            reads=[pBk, 'Ls'], writes=['Ls'])
                tr.op('act', lambda: nc.scalar.activation(out=Ls[:, :, :], in_=Ls[:, :, :], func=AF.Exp, scale=SCALE),
                      reads=['Ls'], writes=['Ls'])
                pO, pOk = psA()
                for h4 in range(4):
                    hd = hh2 * 4 + h4
                    tr.op('pe', lambda h4=h4, hd=hd, pO=pO: nc.tensor.matmul(
                        pO[0:TS, h4 * 8:(h4 + 1) * 8], lhsT=ks32[:, hd, :], rhs=qs32[:, hd, :], start=True, stop=True),
                        reads=['ks32', 'qs32'], writes=[pOk])
                tr.op('act', lambda pO=pO: nc.scalar.activation(
                    out=Pown[0:TS, :, :], in_=pO[0:TS, 0:32].rearrange("p (h q) -> p h q", h=4), func=AF.Exp, scale=SCALE),
                    reads=[pOk], writes=['Pown'])
                for h4 in range(4):
                    tr.op('dve', lambda h4=h4: nc.vector.tensor_tensor(
                        out=Pown[0:TS, h4, :], in0=Pown[0:TS, h4, :], in1=cmask[0:TS, 0:TS], op=ALU.mult),
                        reads=['Pown', 'cmask'], writes=['Pown'])
                if flags.get('sm_stage', 4) < 3:
                    continue
                pD, pDk = psA()
                fns = [lambda p=p, pD=pD: nc.tensor.matmul(pD[:, 0:32], lhsT=ones32[:, 0:128], rhs=Ls[:, p, :],
                                                           start=(p == 0), stop=False) for p in range(128)]
                fns.append(lambda pD=pD: nc.tensor.matmul(
                    pD[:, 0:32], lhsT=ones32[0:TS, 0:128], rhs=Pown[0:TS, :, :].rearrange("p h q -> p (h q)"),
                    start=False, stop=True))
                tr.group('pe', fns, reads=['Ls', 'Pown', 'ones32'], writes=[pDk])
                tr.op('dve', lambda pD=pD: nc.vector.reciprocal(out=rdn[:, :], in_=pD[:, 0:32]), reads=[pDk], writes=['rdn'])
                if flags.get('sm_stage', 4) < 4:
                    continue
                for p in range(128):
                    buf, bk = gather(cv, p)
                    for h4 in range(4):
                        hd = hh2 * 4 + h4
                        tr.op('pe', lambda h4=h4, hd=hd, buf=buf, p=p: nc.tensor.matmul(
                            ps[4 + h4][:, 0:TS], lhsT=buf[:, hd * 128:(hd + 1) * 128], rhs=Ls[:, p, h4 * 8:(h4 + 1) * 8],
                            start=(p == 0), stop=False), reads=[bk, 'Ls'], writes=[('ps', 4 + h4)])
                for h4 in range(4):
                    hd = hh2 * 4 + h4
                    tr.op('pe', lambda h4=h4, hd=hd: nc.tensor.matmul(
                        ps[4 + h4][:, 0:TS], lhsT=vs32[0:TS, hd * 128:(hd + 1) * 128], rhs=Pown[0:TS, h4, :],
                        start=False, stop=True), reads=['vs32', 'Pown'], writes=[('ps', 4 + h4)])
                    tr.op('dve', lambda h4=h4, hd=hd: nc.vector.tensor_tensor(
                        out=oTm[:, hd, TP:T], in0=ps[4 + h4][:, 0:TS], in1=rdn[:, h4 * 8:(h4 + 1) * 8], op=ALU.mult),
                        reads=[('ps', 4 + h4), 'rdn'], writes=['oTm'])

        prog = flags.get('prog', DEFAULT_PROG)
        for li, layer in enumerate(prog):
            for q in range(flags.get('nq', 4)):
                if li == 0:
                    load_x(q)
                else:
                    load_xd(q)
                for (kind, l) in layer:
                    if kind == 'ffn0':
                        ffn(l, 0, l * 3 + 0)
                    elif kind == 'ffn1':
                        ffn(l, 1, l * 3 + 2)
                    elif kind == 'gla':
                        gla_layer(l, l * 3 + 1, q)
                    elif kind == 'even':
                        even_layer(l, l * 3 + 1, q)
                if li == len(prog) - 1:
                    final_out(q)
                else:
                    store_xd(q)
        tr.finish()
    return nc


def _host_inputs(inp, c, flags=None):
    inp = _Lazy(inp)
    b, j = c // 4, c % 4
    norm_all = np.concatenate([inp['norm_w'].reshape(12, D), inp['final_norm_w'].reshape(1, D)], axis=0)
    normT = np.ascontiguousarray(norm_all.reshape(13, NCH, 128).transpose(2, 0, 1).reshape(128, 13 * NCH))
    m = {
        'xp': np.ascontiguousarray(inp['x_prompt'][b]),
        'xs': np.ascontiguousarray(inp['x_sample'][c]),
        'normT': normT,
        'ffn_gate': inp['ffn_gate'], 'ffn_up': inp['ffn_up'], 'ffn_down': inp['ffn_down'],
        'gla_w_in': inp['gla_w_in'], 'gla_w_gate_up': inp['gla_w_gate_up'],
        'gla_b_gate': np.ascontiguousarray(inp['gla_b_gate'].reshape(2, 1, 1024)), 'gla_w_out': inp['gla_w_out'],
        'gnormT': np.ascontiguousarray(inp['gla_norm_w'].reshape(2, 4, 128).transpose(2, 0, 1).reshape(128, 8)),
        'sgla': np.ascontiguousarray(inp['state_gla'][:, c]),
        'even_w_in': inp['even_w_in'], 'even_w_out': inp['even_w_out'],
        'lbrow': np.ascontiguousarray(np.tile(inp['hgrn_lb_logits'].reshape(1, 2048), (128, 1))),
        'lbT': np.ascontiguousarray(inp['hgrn_lb_logits'].reshape(2, 8, 128).transpose(2, 0, 1).reshape(128, 16)),
        'hnormT': np.ascontiguousarray(inp['hgrn_norm_w'].T),
        'shgrn': np.ascontiguousarray(inp['state_hgrn'][:, c]),
        'cache_k': inp['cache_k'].reshape(2, 1280 * 128, 1024),
        'cache_v': inp['cache_v'].reshape(2, 1280 * 128, 1024),
        'ptab': np.ascontiguousarray(np.tile(inp['page_table'][c].reshape(1, 128).astype(np.int32), (128, 1))),
    }
    m = {k: v for k, v in m.items() if isinstance(v, np.ndarray) and v.dtype != object}
    if not (flags or {}).get('sample_moba', SAMPLE_MOBA_DEFAULT):
        for k in ('cache_k', 'cache_v', 'ptab'):
            m.pop(k, None)
    m.setdefault('gnormT', np.zeros((128, 8), np.float32))
    m.setdefault('hnormT', np.zeros((128, 2), np.float32))
    return m


def run(inp, flags, trace=False):
    nc = build(flags)
    in_maps = [_host_inputs(inp, c, flags) for c in range(8)]
    res = run_bass_kernel_spmd(nc, in_maps, core_ids=list(range(8)), trace=trace)
    return res


def kernel(**inp):
    inp = {k: np.asarray(v) for k, v in inp.items()}
    res = run(inp, {})
    r = res.results
    yp = np.stack([r[0]['yp'], r[4]['yp']])
    ys = np.stack([r[c]['ys'] for c in range(8)])
    kp = np.stack([r[0]['kp'], r[4]['kp']], axis=1).reshape(2, 2, 32, 128, 8, 128)
    vp = np.stack([r[0]['vp'], r[4]['vp']], axis=1).reshape(2, 2, 32, 128, 8, 128)
    hp = np.stack([r[0]['hgrn_p'], r[4]['hgrn_p']], axis=1)
    gp = np.stack([r[0]['gla_p'], r[4]['gla_p']], axis=1)
    ks = np.stack([r[c]['ks'] for c in range(8)], axis=1).reshape(2, 8, 8, 8, 128)
    vs = np.stack([r[c]['vs'] for c in range(8)], axis=1).reshape(2, 8, 8, 8, 128)
    hs = np.stack([r[c]['hgrn_s'] for c in range(8)], axis=1)
    gs = np.stack([r[c]['gla_s'] for c in range(8)], axis=1)
    return (yp, ys, kp, vp, hp, gp, ks, vs, hs, gs)
```

```python
import numpy as np
from contextlib import ExitStack
import concourse.bass as bass
import concourse.mybir as mybir
from concourse.bass_utils import run_bass_kernel_spmd

F32 = mybir.dt.float32
BF16 = mybir.dt.bfloat16
I32 = mybir.dt.int32
AF = mybir.ActivationFunctionType
ALU = mybir.AluOpType
AX = mybir.AxisListType

D = 2048
NCH = 16
T = 1032
TP = 1024
TS = 8
TG = [(0, 344), (344, 344), (688, 344)]
DFF = 5632
NG = DFF // 256
EPS = 1e-6
DEPTH = 4
NS = 4
ARENA = 62 * 1024
TT = [(96 * i, 96) for i in range(10)] + [(960, 64), (1024, 8)]
NTT = len(TT)
NDS = 24
SAMPLE_MOBA_DEFAULT = True


class Tr:
    def __init__(self, nc, st):
        self.nc = nc
        self.eng = {'pe': nc.tensor, 'dve': nc.vector, 'act': nc.scalar, 'pool': nc.gpsimd, 'sp': nc.sync}
        self.sem = {k: st.enter_context(nc.semaphore('s_' + k)) for k in self.eng}
        self.cnt = {k: 0 for k in self.eng}
        self.seen = {k: {} for k in self.eng}
        self.lastw = {}
        self.readers = {}
        self.dsem = [st.enter_context(nc.semaphore('d%d' % i)) for i in range(NDS)]
        self.dcnt = [0] * NDS
        self.dnext = 0
        self.psem = [st.enter_context(nc.semaphore('p%d' % i)) for i in range(8)]
        self.pcnt = [0] * 8
        self.pnext = 0
        self.xsem = {}
        self.xcnt = {}
        self.st = st
        self.outtoks = []

    def semh(self, key):
        if isinstance(key, str):
            return self.sem[key]
        if key[0] == 'd':
            return self.dsem[key[1]]
        if key[0] == 'p':
            return self.psem[key[1]]
        return self.xsem[key]

    def _need(self, e, deps):
        for key, val in deps.items():
            if key == e and e in ('pe', 'sp'):
                continue
            if self.seen[e].get(key, 0) >= val:
                continue
            self.eng[e].wait_ge(self.semh(key), val)
            self.seen[e][key] = val

    def deps_for(self, reads, writes):
        deps = {}
        for r in reads:
            w = self.lastw.get(r)
            if w is not None and deps.get(w[0], 0) < w[1]:
                deps[w[0]] = w[1]
        for r in writes:
            w = self.lastw.get(r)
            if w is not None and deps.get(w[0], 0) < w[1]:
                deps[w[0]] = w[1]
            for rd in self.readers.get(r, ()):
                if deps.get(rd[0], 0) < rd[1]:
                    deps[rd[0]] = rd[1]
        return deps

    def _record(self, tok, reads, writes):
        for r in reads:
            self.readers.setdefault(r, []).append(tok)
        for r in writes:
            self.lastw[r] = tok
            self.readers[r] = []

    def op(self, e, fn, reads=(), writes=()):
        self._need(e, self.deps_for(reads, writes))
        ins = fn()
        self.cnt[e] += 1
        ins.then_inc(self.sem[e], 1)
        self._record((e, self.cnt[e]), reads, writes)
        return ins

    def group(self, e, fns, reads=(), writes=()):
        self._need(e, self.deps_for(reads, writes))
        ins = None
        for fn in fns:
            ins = fn()
        self.cnt[e] += 1
        ins.then_inc(self.sem[e], 1)
        self._record((e, self.cnt[e]), reads, writes)

    def newsem(self, name):
        key = ('x', name)
        self.xsem[key] = self.st.enter_context(self.nc.semaphore('x_' + name))
        self.xcnt[key] = 0
        return key

    def dma(self, q, out, in_, reads=(), writes=(), semkey=None, is_out=False, fn=None, inc=16):
        self._need(q, self.deps_for(reads, writes))
        if semkey is None and q == 'pool':
            i = self.pnext
            self.pnext = (i + 1) % 8
            key = ('p', i)
            if self.pcnt[i] > 0:
                self._need(q, {key: self.pcnt[i]})
            self.pcnt[i] += inc
            val = self.pcnt[i]
        elif semkey is None:
            i = self.dnext
            self.dnext = (i + 1) % NDS
            key = ('d', i)
            if self.dcnt[i] > 0:
                self._need(q, {key: self.dcnt[i]})
            self.dcnt[i] += inc
            val = self.dcnt[i]
        else:
            key = semkey
            if self.xcnt[key] > 0:
                self._need(q, {key: self.xcnt[key]})
            self.xcnt[key] += inc
            val = self.xcnt[key]
        if fn is None:
            ins = self.eng[q].dma_start(out=out, in_=in_)
        else:
            ins = fn()
        if inc == 16:
            ins.then_inc(self.semh(key), 16)
        else:
            ins.then_inc(self.semh(key))
        tok = (key, val)
        self._record(tok, reads, writes)
        if is_out:
            self.outtoks.append(tok)
        return tok

    def barrier(self):
        deps = {}
        for k, v in self.cnt.items():
            if v > 0:
                deps[k] = v
        for i in range(NDS):
            if self.dcnt[i] > 0:
                deps[('d', i)] = self.dcnt[i]
        for i in range(8):
            if self.pcnt[i] > 0:
                deps[('p', i)] = self.pcnt[i]
        for k, v in self.xcnt.items():
            if v > 0:
                deps[k] = v
        for e in self.eng:
            self._need(e, deps)

    def epoch(self):
        switched = False
        for e in ('pe', 'dve', 'act'):
            if self.cnt[e] > 12000:
                self.nep = getattr(self, 'nep', 0) + 1
                self.sem[e] = self.st.enter_context(self.nc.semaphore('s_%s_%d' % (e, self.nep)))
                self.cnt[e] = 0
                for k in self.seen:
                    self.seen[k].pop(e, None)
                switched = True
        if switched:
            self.lastw = {}
            self.readers = {}

    def finish(self):
        deps = {}
        for key, val in self.outtoks:
            deps[key] = max(deps.get(key, 0), val)
        self._need('sp', deps)
        self.barrier()


class Ctx:
    pass


class _Lazy(dict):
    def __init__(self, d):
        super().__init__(d)

    def __getitem__(self, k):
        v = dict.get(self, k)
        if v is None:
            return _Zero()
        return v


class _Zero:
    def __getitem__(self, k):
        return self

    def reshape(self, *a):
        return self

    def transpose(self, *a):
        return self

    def astype(self, *a):
        return self

    @property
    def T(self):
        return self


def weight_seq(flags):
    seq = []

    def ffn(l, i):
        pend = None
        for g in range(NG):
            seq.append(('gate', l, i, g))
            seq.append(('up', l, i, g))
            if pend is not None:
                seq.append(('down', l, i, pend))
            pend = g
        seq.append(('down', l, i, pend))

    def gla(o):
        for h in range(4):
            seq.append(('glain', o, h))
            seq.append(('glain', o, 4 + h))
            seq.append(('glain', o, 8 + 2 * h))
            seq.append(('glain', o, 9 + 2 * h))
            seq.append(('glain', o, 16 + 2 * h))
            seq.append(('glain', o, 17 + 2 * h))
            seq.append(('glaout', o, 2 * h))
            seq.append(('glaout', o, 2 * h + 1))

    def even(e):
        for hp in range(4 if flags.get('ev_hgrn', True) else 0):
            for blk in (0, 4, 8, 12):
                seq.append(('evin', e, blk + hp))
            seq.append(('evout', e, hp))
        for mp in range(4 if flags.get('ev_moba', True) else 0):
            for blk in (16, 20, 24):
                seq.append(('evin', e, blk + mp))
        for mp in range(4 if flags.get('ev_moba', True) else 0):
            seq.append(('evout', e, 4 + mp))

    for layer in flags.get('prog', DEFAULT_PROG):
        for q in range(flags.get('nq', 4)):
            for (kind, l) in layer:
                if kind == 'even':
                    even(l // 2)
                if kind == 'ffn0':
                    ffn(l, 0)
                elif kind == 'ffn1':
                    ffn(l, 1)
                elif kind == 'gla':
                    gla(l // 2)
    return seq


_F = {'ffn0', 'ffn1'}
_G = {'gla'}
_E = {'even'}
IN_GROUP = {'ffn_gate': _F, 'ffn_up': _F, 'ffn_down': _F,
            'gla_w_in': _G, 'gla_w_gate_up': _G, 'gla_b_gate': _G, 'gla_w_out': _G, 'sgla': _G,
            'even_w_in': _E, 'even_w_out': _E, 'lbrow': _E, 'lbT': _E, 'shgrn': _E,
            'cache_k': _E, 'cache_v': _E, 'ptab': _E}
DEFAULT_PROG = []
for _l in range(DEPTH):
    DEFAULT_PROG.append([('ffn0', _l), ('gla' if _l % 2 else 'even', _l), ('ffn1', _l)])


def build(flags):
    nc = bass.Bass("TRN2", target_bir_lowering=False)
    K = Ctx()
    K.nc = nc
    K.flags = flags
    dr = {}

    def din(name, shape, dt=F32):
        grp = IN_GROUP.get(name)
        if grp is not None and not (kinds & grp):
            return None
        dr[name] = nc.dram_tensor(name, list(shape), dt, kind="ExternalInput").ap()
        return dr[name]

    def dout(name, shape, dt=F32):
        dr[name] = nc.dram_tensor(name, list(shape), dt, kind="ExternalOutput").ap()
        return dr[name]

    kinds = set(k for layer in flags.get('prog', DEFAULT_PROG) for k, _ in layer)
    din('xp', [4 * TP, D])
    din('xs', [TS, D])
    din('normT', [128, 13 * NCH])
    din('ffn_gate', [DEPTH, 2, D, DFF])
    din('ffn_up', [DEPTH, 2, D, DFF])
    din('ffn_down', [DEPTH, 2, DFF, D])
    din('gla_w_in', [2, D, 6160])
    din('gla_w_gate_up', [2, 16, 1024])
    din('gla_b_gate', [2, 1, 1024])
    din('gla_w_out', [2, D, D])
    din('gnormT', [128, 8])
    din('even_w_in', [2, D, 7168])
    din('even_w_out', [2, D, D])
    din('lbrow', [128, 2048])
    din('lbT', [128, 16])
    din('hnormT', [128, 2])
    din('shgrn', [2, 8, 128, 128])
    if flags.get('sample_moba', SAMPLE_MOBA_DEFAULT):
        din('cache_k', [2, 1280 * 128, 1024])
        din('cache_v', [2, 1280 * 128, 1024])
        din('ptab', [128, 128], I32)
    dout('hgrn_p', [2, 8, 128, 128])
    dout('hgrn_s', [2, 8, 128, 128])
    dout('kp', [2, 4 * TP, 1024])
    dout('vp', [2, 4 * TP, 1024])
    dout('ks', [2, TS, 1024])
    dout('vs', [2, TS, 1024])
    KTd = nc.dram_tensor('KTd', [8, 128, 4 * TP], BF16).ap()
    din('sgla', [2, 4, 256, 512])
    dout('gla_p', [2, 4, 256, 512])
    dout('gla_s', [2, 4, 256, 512])
    Xd = nc.dram_tensor('Xd', [4, 128, NCH * TP], F32).ap()
    Xs = nc.dram_tensor('Xs', [128, NCH * TS], F32).ap()
    dout('yp', [4 * TP, D])
    dout('ys', [TS, D])
    if flags.get('dbg'):
        dout('dbg_h', [128, NCH * T], BF16)
        dout('dbg_a', [128, 2 * T], BF16)
        dout('dbg_x', [128, NCH * T], F32)
        dout('dbg_p', [128, 3 * 344], F32)

    with ExitStack() as st:
        tr = Tr(nc, st)
        K.tr = tr

        def sb(name, shape, dt):
            return st.enter_context(nc.sbuf_tensor('sb_' + name, list(shape), dt))

        xT = sb('xT', [128, NCH, T], F32)
        hT = sb('hT', [128, NCH, T], BF16)
        ring = [sb('ring%d' % i, [128, 4096], BF16) for i in range(NS)]
        arena = sb('arena', [128, ARENA // 2], BF16)
        rmask = sb('rmask', [128, T], F32)
        cmask = sb('cmask', [128, 32], F32)
        revtri = sb('revtri', [128, 128], F32)
        onec = sb('onec', [128, 1], F32)
        gnw = sb('gnw', [128, 8], F32)
        hnw = sb('hnw', [128, 2], F32)
        kmT = sb('kmT', [128, 8, 16], BF16)
        esel = sb('esel', [32, 16, 128], BF16)
        mskA = sb('mskA', [128, 256], F32)
        mskB = sb('mskB', [128, 256], F32)
        negc = sb('negc', [128, 24], F32)

        class Arena:
            def __init__(self):
                self.off = 0

            def reset(self):
                tr.barrier()
                tr.epoch()
                self.off = 0

            def __call__(self, shape, dt):
                n = 1
                for x in shape:
                    n *= x
                nb = n * (4 if dt == F32 else 2)
                nb = (nb + 31) // 32 * 32
                assert self.off + nb <= ARENA, (self.off, nb)
                v = arena[:, self.off // 2:(self.off + nb) // 2]
                self.off += nb
                if dt == F32:
                    v = v.bitcast(F32)
                v = v[:, 0:n]
                if len(shape) == 2:
                    v = v.rearrange("p (a b) -> p a b", a=shape[0])
                elif len(shape) == 3:
                    v = v.rearrange("p (a b c) -> p a b c", a=shape[0], b=shape[1])
                return v

        AR = Arena()
        aT = [None, None]
        stmp = [None, None, None]
        dtmp = [None, None, None]
        io = [None, None]

        def alloc_ffn():
            AR.reset()
            for i in range(2):
                aT[i] = AR([2, T], BF16)
            for i in range(3):
                stmp[i] = AR([344], F32)
            for i in range(3):
                dtmp[i] = AR([344], F32)

        def alloc_io():
            AR.reset()
            for i in range(2):
                io[i] = AR([D], F32)
        rstd = sb('rstd', [128, T], F32)
        nw = sb('nw', [128, 13 * NCH], F32)
        ones_bf = sb('ones_bf', [128, 128], BF16)
        ident = sb('ident', [128, 128], F32)
        epsc = sb('epsc', [128, 1], F32)
        ps = [st.enter_context(nc.psum_tensor('ps%d' % i, [128, 512], F32)) for i in range(8)]
        wsem = [tr.newsem('w%d' % i) for i in range(NS)]

        tr.op('pool', lambda: nc.gpsimd.memset(ones_bf[:], 1.0), writes=['ones'])
        tr.op('pool', lambda: nc.gpsimd.memset(ident[:], 0.0), writes=['ident'])
        tr.op('pool', lambda: nc.gpsimd.affine_select(out=ident[:], in_=ident[:], pattern=[[-1, 128]],
                                                      compare_op=ALU.not_equal, fill=1.0, base=0,
                                                      channel_multiplier=1),
              reads=['ident'], writes=['ident'])
        tr.dma('sp', nw[:], dr['normT'][:, :], writes=['nw'])
        tr.op('pool', lambda: nc.gpsimd.memset(epsc[:], EPS), writes=['epsc'])
        tr.op('pool', lambda: nc.gpsimd.memset(onec[:], 1.0), writes=['onec'])
        tr.op('pool', lambda: nc.gpsimd.memset(rmask[:], 1.0), writes=['rmask'])
        tr.op('pool', lambda: nc.gpsimd.memset(
            rmask[:, 0:1024].rearrange("p (j s) -> p j s", s=32)[:, :, 0:1], 0.0), reads=['rmask'], writes=['rmask'])
        tr.op('pool', lambda: nc.gpsimd.memset(rmask[:, 1024:1025], 0.0), reads=['rmask'], writes=['rmask'])
        tr.op('pool', lambda: nc.gpsimd.memset(cmask[:], 1.0), writes=['cmask'])
        tr.op('pool', lambda: nc.gpsimd.memset(revtri[:], 0.0), writes=['revtri'])
        for blk in range(4):
            b0 = blk * 32
            tr.op('pool', lambda b0=b0: nc.gpsimd.affine_select(
                out=cmask[b0:b0 + 32, :], in_=cmask[b0:b0 + 32, :], pattern=[[1, 32]], compare_op=ALU.is_ge,
                fill=0.0, base=0, channel_multiplier=-1), reads=['cmask'], writes=['cmask'])
            tr.op('pool', lambda b0=b0: nc.gpsimd.memset(revtri[b0:b0 + 32, b0:b0 + 32], 1.0),
                  reads=['revtri'], writes=['revtri'])
            tr.op('pool', lambda b0=b0: nc.gpsimd.affine_select(
                out=revtri[b0:b0 + 32, b0:b0 + 32], in_=revtri[b0:b0 + 32, b0:b0 + 32], pattern=[[-1, 32]],
                compare_op=ALU.is_gt, fill=0.0, base=0, channel_multiplier=1), reads=['revtri'], writes=['revtri'])
        tr.dma('sp', gnw[:], dr['gnormT'][:, :], writes=['gnw'])
        tr.dma('sp', hnw[:], dr['hnormT'][:, :], writes=['hnw'])
        tr.op('pool', lambda: nc.gpsimd.memset(negc[:], -1.0e30), writes=['negc'])
        tr.op('pool', lambda: nc.gpsimd.memset(esel[:], 1.0), writes=['esel'])
        tr.op('pool', lambda: nc.gpsimd.affine_select(
            out=esel[:], in_=esel[:], pattern=[[-1, 16], [0, 128]], compare_op=ALU.is_equal, fill=0.0, base=0,
            channel_multiplier=1), reads=['esel'], writes=['esel'])
        tr.op('pool', lambda: nc.gpsimd.memset(mskA[:], 1.0), writes=['mskA'])
        tr.op('pool', lambda: nc.gpsimd.affine_select(
            out=mskA[:, 0:128], in_=mskA[:, 0:128], pattern=[[1, 128]], compare_op=ALU.is_ge, fill=0.0, base=0,
            channel_multiplier=-1), reads=['mskA'], writes=['mskA'])
        tr.op('pool', lambda: nc.gpsimd.memset(mskB[:], 0.0), writes=['mskB'])
        tr.op('pool', lambda: nc.gpsimd.memset(mskB[:, 128:256], 1.0), reads=['mskB'], writes=['mskB'])
        tr.op('pool', lambda: nc.gpsimd.affine_select(
            out=mskB[:, 128:256], in_=mskB[:, 128:256], pattern=[[1, 128]], compare_op=ALU.is_ge, fill=0.0, base=0,
            channel_multiplier=-1), reads=['mskB'], writes=['mskB'])

        wseq = weight_seq(flags)
        wstate = {'issued': 0, 'used': 0, 'floor': 0, 'rel': set()}

        def wsrc(desc):
            kind = desc[0]
            if kind in ('gate', 'up'):
                _, l, i, g = desc
                src = dr['ffn_' + kind][l, i][:, g * 256:(g + 1) * 256].rearrange("(c p) f -> p c f", p=128)
                return src, 'cols'
            if kind == 'down':
                _, l, i, g = desc
                src = dr['ffn_down'][l, i][g * 256:(g + 1) * 256, :].rearrange("(j p) m -> p j m", p=128)
                return src, 'rows'
            if kind == 'glain':
                _, o, blk = desc
                src = dr['gla_w_in'][o][:, blk * 256:(blk + 1) * 256].rearrange("(c p) f -> p c f", p=128)
                return src, 'cols'
            if kind == 'evin':
                _, e, blk = desc
                src = dr['even_w_in'][e][:, blk * 256:(blk + 1) * 256].rearrange("(c p) f -> p c f", p=128)
                return src, 'cols'
            if kind == 'evout':
                _, e, rt = desc
                src = dr['even_w_out'][e][rt * 256:(rt + 1) * 256, :].rearrange("(j p) m -> p j m", p=128)
                return src, 'rows'
            if kind == 'glaout':
                _, o, rt = desc
                src = dr['gla_w_out'][o][rt * 256:(rt + 1) * 256, :].rearrange("(j p) m -> p j m", p=128)
                return src, 'rows'
            raise ValueError(kind)

        def wview(slot, lay):
            if lay == 'cols':
                return ring[slot][:].rearrange("p (c f) -> p c f", c=16)
            return ring[slot][:].rearrange("p (j m) -> p j m", j=2)

        def wissue():
            n = wstate['issued']
            desc = wseq[n]
            slot = n % NS
            src, lay = wsrc(desc)
            tr.dma('pool', wview(slot, lay), src, writes=[('w', slot)], semkey=wsem[slot])
            wstate['issued'] += 1

        def wpump():
            while wstate['issued'] < len(wseq) and wstate['issued'] - NS < wstate['floor']:
                wissue()

        def wnext(desc):
            n = wstate['used']
            assert wseq[n] == desc, (wseq[n], desc)
            wpump()
            assert wstate['issued'] > n
            wstate['used'] += 1
            slot = n % NS
            _, lay = wsrc(desc)
            return wview(slot, lay), ('w', slot), n

        def wrel(n):
            wstate['rel'].add(n)
            while wstate['floor'] in wstate['rel']:
                wstate['rel'].discard(wstate['floor'])
                wstate['floor'] += 1
            wpump()

        psrr = {'A': 0, 'B': 0}

        def psA():
            i = psrr['A'] % 4
            psrr['A'] += 1
            return ps[i], ('ps', i)

        def psB():
            i = 4 + psrr['B'] % 4
            psrr['B'] += 1
            return ps[i], ('ps', i)

        xkeys = [('xT', c) for c in range(NCH)]
        hkeys = [('hT', c) for c in range(NCH)]

        def load_x(q):
            alloc_io()
            for i in range(9):
                buf = io[i % 2]
                bkey = ('io', i % 2)
                if i < 8:
                    n = 128
                    tr.dma('sp', buf[:, :], dr['xp'][q * TP + i * 128:q * TP + (i + 1) * 128, :], writes=[bkey])
                else:
                    n = TS
                    tr.dma('sp', buf[0:TS, :], dr['xs'][:, :], writes=[bkey])
                for cg in range(4):
                    pt, pk = psA()
                    fns = []
                    for cc in range(4):
                        c = cg * 4 + cc
                        fns.append(lambda c=c, cc=cc, pt=pt, n=n, buf=buf: nc.tensor.transpose(
                            out=pt[:, cc * 128:cc * 128 + n], in_=buf[0:n, c * 128:(c + 1) * 128],
                            identity=ident[0:n, 0:n]))
                    for fn in fns:
                        tr.op('pe', fn, reads=[bkey, 'ident'], writes=[pk])
                    src = pt[:].rearrange("p (c n) -> p c n", c=4)[:, :, 0:n]
                    dst = xT[:, cg * 4:(cg + 1) * 4, i * 128:i * 128 + n]
                    tr.op('dve', lambda dst=dst, src=src: nc.vector.tensor_copy(out=dst, in_=src),
                          reads=[pk], writes=[('xT', cg * 4 + k) for k in range(4)])

        def compute_rstd(width):
            for c in range(NCH):
                tr.op('act', lambda c=c: nc.scalar.activation(out=hT[:, c, :], in_=xT[:, c, :], func=AF.Square),
                      reads=[('xT', c)], writes=[('hT', c)])
            for t, (t0, tn) in enumerate(TG):
                pt, pk = psB()
                fns = [(lambda c=c, pt=pt, t0=t0, tn=tn: nc.tensor.matmul(
                    pt[:, 0:tn], lhsT=ones_bf[:, :], rhs=hT[:, c, t0:t0 + tn], start=(c == 0), stop=(c == NCH - 1)))
                    for c in range(NCH)]
                tr.group('pe', fns, reads=hkeys + ['ones'], writes=[pk])
                tr.op('act', lambda pt=pt, t0=t0, tn=tn: nc.scalar.activation(
                    out=rstd[:, t0:t0 + tn], in_=pt[:, 0:tn], func=AF.Sqrt, bias=epsc[:, 0:1], scale=1.0 / width),
                    reads=[pk, 'epsc'], writes=[('rstd', t)])
                tr.op('dve', lambda t0=t0, tn=tn: nc.vector.reciprocal(
                    out=rstd[:, t0:t0 + tn], in_=rstd[:, t0:t0 + tn]), reads=[('rstd', t)], writes=[('rstd', t)])

        rkeys = [('rstd', t) for t in range(3)]

        def rmsnorm(nidx):
            compute_rstd(float(D))
            for c in range(NCH):
                tr.op('dve', lambda c=c: nc.vector.scalar_tensor_tensor(
                    out=hT[:, c, :], in0=xT[:, c, :], scalar=nw[:, nidx * NCH + c:nidx * NCH + c + 1],
                    in1=rstd[:, :], op0=ALU.mult, op1=ALU.mult),
                    reads=[('xT', c), 'nw'] + rkeys, writes=[('hT', c)])

        def ffn(l, i, nidx):
            alloc_ffn()
            rmsnorm(nidx)
            pend = None

            def down(g, l=l, i=i):
                wd, wk, wn = wnext(('down', l, i, g))
                a = aT[g % 2]
                for m in range(NCH):
                    for t, (t0, tn) in enumerate(TG):
                        pt, pk = psB()
                        fns = [(lambda j=j, pt=pt, m=m, t0=t0, tn=tn: nc.tensor.matmul(
                            pt[:, 0:tn], lhsT=wd[:, j, m * 128:(m + 1) * 128], rhs=a[:, j, t0:t0 + tn],
                            start=(j == 0), stop=(j == 1))) for j in range(2)]
                        tr.group('pe', fns, reads=[wk, ('aT', g % 2, 0), ('aT', g % 2, 1)], writes=[pk])
                        u = m * 3 + t
                        if u % 3 == 2:
                            di = (u // 3) % 3
                            d_ = dtmp[di]
                            tr.op('act', lambda pt=pt, d_=d_, tn=tn: nc.scalar.activation(
                                out=d_[:, 0:tn], in_=pt[:, 0:tn], func=AF.Identity, scale=0.5),
                                reads=[pk], writes=[('dtmp', di)])
                            tr.op('pool', lambda d_=d_, m=m, t0=t0, tn=tn: nc.gpsimd.tensor_tensor(
                                out=xT[:, m, t0:t0 + tn], in0=xT[:, m, t0:t0 + tn], in1=d_[:, 0:tn], op=ALU.add),
                                reads=[('dtmp', di), ('xT', m)], writes=[('xT', m)])
                        else:
                            tr.op('dve', lambda pt=pt, m=m, t0=t0, tn=tn: nc.vector.scalar_tensor_tensor(
                                out=xT[:, m, t0:t0 + tn], in0=pt[:, 0:tn], scalar=0.5, in1=xT[:, m, t0:t0 + tn],
                                op0=ALU.mult, op1=ALU.add), reads=[pk, ('xT', m)], writes=[('xT', m)])
                wrel(wn)

            for g in range(NG):
                wg, wgk, wgn = wnext(('gate', l, i, g))
                wu, wuk, wun = wnext(('up', l, i, g))
                a = aT[g % 2]
                for j in range(2):
                    for t, (t0, tn) in enumerate(TG):
                        pg, pgk = psA()
                        pu, puk = psA()
                        for (w_, wk_, p_, pk_) in ((wg, wgk, pg, pgk), (wu, wuk, pu, puk)):
                            fns = [(lambda c=c, w_=w_, p_=p_, t0=t0, tn=tn, j=j: nc.tensor.matmul(
                                p_[:, 0:tn], lhsT=w_[:, c, j * 128:(j + 1) * 128], rhs=hT[:, c, t0:t0 + tn],
                                start=(c == 0), stop=(c == NCH - 1))) for c in range(NCH)]
                            tr.group('pe', fns, reads=[wk_] + hkeys, writes=[pk_])
                        si = (j * 3 + t) % 3
                        s_ = stmp[si]
                        if flags.get('dbg') and g == 0 and j == 0 and t == 0 and l == 0 and i == 0:
                            dbt = sb('dbt', [128, 3 * 344], F32)
                            tr.op('dve', lambda: nc.vector.tensor_copy(out=dbt[:, 0:344], in_=pg[:, 0:344]), reads=[pgk], writes=['dbt'])
                            tr.op('dve', lambda: nc.vector.tensor_copy(out=dbt[:, 344:688], in_=pu[:, 0:344]), reads=[puk], writes=['dbt'])
                        tr.op('act', lambda s_=s_, pg=pg, tn=tn: nc.scalar.activation(
                            out=s_[:, 0:tn], in_=pg[:, 0:tn], func=AF.Silu), reads=[pgk], writes=[('stmp', si)])
                        tr.op('dve', lambda s_=s_, pu=pu, a=a, j=j, t0=t0, tn=tn: nc.vector.tensor_tensor(
                            out=a[:, j, t0:t0 + tn], in0=s_[:, 0:tn], in1=pu[:, 0:tn], op=ALU.mult),
                            reads=[puk, ('stmp', si)], writes=[('aT', g % 2, j)])
                        if flags.get('dbg') and g == 0 and j == 0 and t == 0 and l == 0 and i == 0:
                            tr.op('dve', lambda: nc.vector.tensor_copy(out=dbt[:, 688:1032], in_=s_[:, 0:344]), reads=[('stmp', si)], writes=['dbt'])
                            tr.dma('sp', dr['dbg_p'][:, :], dbt[:, :], reads=['dbt'], is_out=True)
                wrel(wgn)
                wrel(wun)
                if flags.get('dbg') and g == 0 and l == 0 and i == 0:
                    tr.dma('sp', dr['dbg_h'][:, :], hT[:].rearrange("p c t -> p (c t)"), reads=hkeys, is_out=True)
                    tr.dma('sp', dr['dbg_a'][:, :], aT[0][:].rearrange("p c t -> p (c t)"), reads=[('aT', 0, 0), ('aT', 0, 1)], is_out=True)
                if pend is not None:
                    down(pend)
                pend = g
                if flags.get('dbg') and g == 1 and l == 0 and i == 0:
                    tr.dma('sp', dr['dbg_x'][:, :], xT[:].rearrange("p c t -> p (c t)"), reads=xkeys, is_out=True)
            down(pend)

        def final_out(q):
            alloc_io()
            compute_rstd(float(D))
            for c in range(NCH):
                tr.op('dve', lambda c=c: nc.vector.scalar_tensor_tensor(
                    out=xT[:, c, :], in0=xT[:, c, :], scalar=nw[:, 12 * NCH + c:12 * NCH + c + 1],
                    in1=rstd[:, :], op0=ALU.mult, op1=ALU.mult),
                    reads=[('xT', c), 'nw'] + rkeys, writes=[('xT', c)])
            for i in range(9):
                buf = io[i % 2]
                bkey = ('io', i % 2)
                n = 128 if i < 8 else TS
                for cg in range(4):
                    pt, pk = psA()
                    for cc in range(4):
                        c = cg * 4 + cc
                        tr.op('pe', lambda c=c, cc=cc, pt=pt, n=n, i=i: nc.tensor.transpose(
                            out=pt[0:n, cc * 128:(cc + 1) * 128], in_=xT[:, c, i * 128:i * 128 + n],
                            identity=ident[:, :]), reads=[('xT', c), 'ident'], writes=[pk])
                    tr.op('dve', lambda pt=pt, n=n, buf=buf, cg=cg: nc.vector.tensor_copy(
                        out=buf[0:n, cg * 512:(cg + 1) * 512], in_=pt[0:n, :]), reads=[pk], writes=[bkey])
                if i < 8:
                    tr.dma('sp', dr['yp'][q * TP + i * 128:q * TP + (i + 1) * 128, :], buf[:, :], reads=[bkey], is_out=True)
                elif q == 0:
                    tr.dma('sp', dr['ys'][:, :], buf[0:TS, :], reads=[bkey], is_out=True)

        def load_xd(q):
            tr.dma('sp', xT[:, :, 0:TP], Xd[q].rearrange("p (c t) -> p c t", c=NCH), reads=[('Xd', q)], writes=xkeys)
            tr.dma('sp', xT[:, :, TP:T], Xs[:, :].rearrange("p (c t) -> p c t", c=NCH), reads=['Xs'], writes=xkeys)

        def store_xd(q):
            tr.dma('sp', Xd[q].rearrange("p (c t) -> p c t", c=NCH), xT[:, :, 0:TP], reads=xkeys, writes=[('Xd', q)])
            if q == 0:
                tr.dma('sp', Xs[:, :].rearrange("p (c t) -> p c t", c=NCH), xT[:, :, TP:T], reads=xkeys, writes=['Xs'])


        PROMPT_CHUNKS = [(j // 3, 32 * (j % 3), 32, 32 * j, j) for j in range(32)]
        SAMPLE_CHUNK = [(NTT - 1, 0, TS, TP, 32)]

        def glr_state_only(R):
            KC, dv = R['KC'], R['dv']
            for (tile, pb, cs, col0, j) in PROMPT_CHUNKS:
                for kc in range(KC):
                    p_, pk = psA()
                    tr.group('pe', [lambda p_=p_, kc=kc, pb=pb, cs=cs, tile=tile: nc.tensor.matmul(
                        p_[:, 0:dv], lhsT=R['khat'][pb:pb + cs, tile, kc * 128:(kc + 1) * 128],
                        rhs=R['vtm'][pb:pb + cs, tile, 0:dv], start=True, stop=True)],
                        reads=['khat', 'vtm'], writes=[pk])
                    tr.op('dve', lambda p_=p_, kc=kc, j=j: nc.vector.scalar_tensor_tensor(
                        out=R['S'][:, kc, :], in0=R['S'][:, kc, :], scalar=R['dT'][:, kc, j:j + 1],
                        in1=p_[:, 0:dv], op0=ALU.mult, op1=ALU.add),
                        reads=[pk, ('S', kc), 'dT'], writes=[('S', kc)])

        def glr_full(R, chunks):
            KC, dv = R['KC'], R['dv']
            VC = dv // 128
            for (tile, pb, cs, col0, j) in chunks:
                pa, pak = psA()
                tr.group('pe', [lambda kc=kc, pa=pa, pb=pb, cs=cs, col0=col0: nc.tensor.matmul(
                    pa[pb:pb + cs, 0:cs], lhsT=R['kT'][:, kc, col0:col0 + cs], rhs=R['qT'][:, kc, col0:col0 + cs],
                    start=(kc == 0), stop=(kc == KC - 1)) for kc in range(KC)],
                    reads=['kT', 'qT'], writes=[pak])
                tr.op('dve', lambda pa=pa, pb=pb, cs=cs: nc.vector.tensor_tensor(
                    out=R['A'][pb:pb + cs, 0:cs], in0=pa[pb:pb + cs, 0:cs], in1=cmask[pb:pb + cs, 0:cs],
                    op=ALU.mult), reads=[pak, 'cmask'], writes=['A'])
                po, pok = psA()
                fns = []
                for vc in range(VC):
                    fns.append(lambda vc=vc, po=po, pb=pb, cs=cs, tile=tile: nc.tensor.matmul(
                        po[:, vc * 32:vc * 32 + cs], lhsT=R['vtm'][pb:pb + cs, tile, vc * 128:(vc + 1) * 128],
                        rhs=R['A'][pb:pb + cs, 0:cs], start=True, stop=False))
                    for kc in range(KC):
                        fns.append(lambda vc=vc, kc=kc, po=po, cs=cs, col0=col0: nc.tensor.matmul(
                            po[:, vc * 32:vc * 32 + cs], lhsT=R['Sbf'][:, kc, vc * 128:(vc + 1) * 128],
                            rhs=R['qT'][:, kc, col0:col0 + cs], start=False, stop=(kc == KC - 1)))
                tr.group('pe', fns, reads=['vtm', 'A', 'qT'] + [('Sbf', kc) for kc in range(KC)], writes=[pok])
                tr.op('act', lambda po=po, cs=cs, col0=col0: nc.scalar.copy(
                    out=R['oT'][:, :, col0:col0 + cs],
                    in_=po[:, 0:VC * 32].rearrange("p (v s) -> p v s", s=32)[:, :, 0:cs]),
                    reads=[pok], writes=['oT'])
                for kc in range(KC):
                    p_, pk = psA()
                    tr.group('pe', [lambda p_=p_, kc=kc, pb=pb, cs=cs, tile=tile: nc.tensor.matmul(
                        p_[:, 0:dv], lhsT=R['khat'][pb:pb + cs, tile, kc * 128:(kc + 1) * 128],
                        rhs=R['vtm'][pb:pb + cs, tile, 0:dv], start=True, stop=True)],
                        reads=['khat', 'vtm'], writes=[pk])
                    tr.op('dve', lambda p_=p_, kc=kc, j=j: nc.vector.scalar_tensor_tensor(
                        out=R['S'][:, kc, :], in0=R['S'][:, kc, :], scalar=R['dT'][:, kc, j:j + 1],
                        in1=p_[:, 0:dv], op0=ALU.mult, op1=ALU.add),
                        reads=[pk, ('S', kc), 'dT'], writes=[('S', kc)])
                    tr.op('act', lambda kc=kc: nc.scalar.copy(out=R['Sbf'][:, kc, :], in_=R['S'][:, kc, :]),
                          reads=[('S', kc)], writes=[('Sbf', kc)])

        def glr_head(R, q, pkey, s_in_ap, p_ap, s_out_ap):
            KC = R['KC']
            Skeys = [('S', kc) for kc in range(KC)]

            def tobf():
                for kc in range(KC):
                    tr.op('act', lambda kc=kc: nc.scalar.copy(out=R['Sbf'][:, kc, :], in_=R['S'][:, kc, :]),
                          reads=[('S', kc)], writes=[('Sbf', kc)])
            if q == 0:
                tr.op('pool', lambda: nc.gpsimd.memset(R['S'][:], 0.0), reads=Skeys, writes=Skeys)
            else:
                tr.dma('sp', R['S'][:], p_ap, reads=[pkey], writes=Skeys)
            tobf()
            glr_full(R, PROMPT_CHUNKS)
            tr.dma('sp', p_ap, R['S'][:], reads=Skeys, writes=[pkey], is_out=True)
            if q == 0:
                tr.dma('sp', R['S'][:], s_in_ap, writes=Skeys)
                tobf()
                glr_full(R, SAMPLE_CHUNK)
                tr.dma('sp', s_out_ap, R['S'][:], reads=Skeys, is_out=True)
            else:
                tr.op('pool', lambda: nc.gpsimd.memset(R['oT'][:, :, TP:T], 0.0), reads=['oT'], writes=['oT'])

        def proj_fm(w_, wk_, col, evac):
            for t, (t0, tn) in enumerate(TG):
                pt, pk = psA()
                tr.group('pe', [lambda c=c, pt=pt, t0=t0, tn=tn: nc.tensor.matmul(
                    pt[:, 0:tn], lhsT=w_[:, c, col:col + 128], rhs=hT[:, c, t0:t0 + tn],
                    start=(c == 0), stop=(c == NCH - 1)) for c in range(NCH)], reads=[wk_] + hkeys, writes=[pk])
                evac(pt, t0, tn, pk)

        def proj_tm(w_, wk_, i, ncols, evac):
            c0, n = TT[i]
            pt, pk = psA()
            tr.group('pe', [lambda c=c, pt=pt: nc.tensor.matmul(
                pt[0:n, 0:ncols], lhsT=hT[:, c, c0:c0 + n], rhs=w_[:, c, 0:ncols],
                start=(c == 0), stop=(c == NCH - 1)) for c in range(NCH)], reads=[wk_] + hkeys, writes=[pk])
            evac(pt, n, pk)

        def outproj_rows(desc, oT_, okey, vc0, scale=1.0):
            wo, wok, won = wnext(desc)
            for m in range(NCH):
                for t, (t0, tn) in enumerate(TG):
                    pt, pk = psB()
                    tr.group('pe', [lambda jj=jj, pt=pt, m=m, t0=t0, tn=tn: nc.tensor.matmul(
                        pt[:, 0:tn], lhsT=wo[:, jj, m * 128:(m + 1) * 128], rhs=oT_[:, vc0 + jj, t0:t0 + tn],
                        start=(jj == 0), stop=(jj == 1)) for jj in range(2)], reads=[wok, okey], writes=[pk])
                    tr.op('dve', lambda pt=pt, m=m, t0=t0, tn=tn: nc.vector.scalar_tensor_tensor(
                        out=xT[:, m, t0:t0 + tn], in0=pt[:, 0:tn], scalar=scale, in1=xT[:, m, t0:t0 + tn],
                        op0=ALU.mult, op1=ALU.add), reads=[pk, ('xT', m)], writes=[('xT', m)])
            wrel(won)

        def gla_layer(l, nidx, q):
            o = l // 2
            AR.reset()
            rmsnorm(nidx)
            wa = AR([16, 16], BF16)
            wupb = AR([1024], BF16)
            aT17 = AR([T], BF16)
            R = {'KC': 2, 'dv': 512}
            qk = AR([4, T], BF16)
            R['qT'] = qk[:, 0:2, :]
            R['kT'] = qk[:, 2:4, :]
            R['khat'] = AR([NTT, 256], BF16)
            R['vtm'] = AR([NTT, 512], BF16)
            R['dT'] = AR([2, 33], F32)
            R['S'] = AR([2, 512], F32)
            R['Sbf'] = AR([2, 512], BF16)
            R['oT'] = AR([4, T], BF16)
            X1 = AR([T], F32)
            X2 = AR([T], F32)
            X3 = AR([T], F32)
            tmpA = AR([256], F32)
            tmpB = AR([256], F32)
            R['A'] = AR([32], BF16)
            st2 = [AR([344], F32) for _ in range(2)]
            tr.dma('pool', wa[:], dr['gla_w_in'][o][:, 6144:6160].rearrange("(c p) f -> p c f", p=128), writes=['wa'])
            tr.dma('pool', wupb[0:16, :], dr['gla_w_gate_up'][o][:, :], writes=['wupb'])
            tr.dma('pool', wupb[16:17, :], dr['gla_b_gate'][o][:, :], writes=['wupb'])
            tr.op('pool', lambda: nc.gpsimd.memset(aT17[0:32, :], 1.0), writes=['aT17'])
            for t, (t0, tn) in enumerate(TG):
                pt, pk = psA()
                tr.group('pe', [lambda c=c, pt=pt, t0=t0, tn=tn: nc.tensor.matmul(
                    pt[0:16, 0:tn], lhsT=wa[:, c, 0:16], rhs=hT[:, c, t0:t0 + tn],
                    start=(c == 0), stop=(c == NCH - 1)) for c in range(NCH)], reads=['wa'] + hkeys, writes=[pk])
                tr.op('dve', lambda pt=pt, t0=t0, tn=tn: nc.vector.tensor_copy(
                    out=aT17[0:16, t0:t0 + tn], in_=pt[0:16, 0:tn]), reads=[pk, 'aT17'], writes=['aT17'])

            for h in range(4):
                wq, wqk, wqn = wnext(('glain', o, h))
                wkt, wkk, wkn = wnext(('glain', o, 4 + h))
                for kc in range(2):
                    ch0 = h * 256 + kc * 128
                    for t, (t0, tn) in enumerate(TG):
                        pt, pk = psA()
                        tr.group('pe', [lambda pt=pt, t0=t0, tn=tn, ch0=ch0: nc.tensor.matmul(
                            pt[:, 0:tn], lhsT=wupb[0:17, ch0:ch0 + 128], rhs=aT17[0:17, t0:t0 + tn],
                            start=True, stop=True)], reads=['wupb', 'aT17'], writes=[pk])
                        tr.op('act', lambda pt=pt, t0=t0, tn=tn: nc.scalar.activation(
                            out=X1[:, t0:t0 + tn], in_=pt[:, 0:tn], func=AF.Exp, scale=-1.0), reads=[pk], writes=['X1'])
                    tr.op('act', lambda: nc.scalar.activation(out=X1[:, :], in_=X1[:, :], func=AF.Ln, bias=onec[:, 0:1]),
                          reads=['X1', 'onec'], writes=['X1'])
                    tr.op('dve', lambda: nc.vector.tensor_tensor_scan(
                        out=X2[:, :], data0=rmask[:, :], data1=X1[:, :], initial=0.0, op0=ALU.mult, op1=ALU.add),
                        reads=['X1', 'rmask'], writes=['X2'])
                    tr.op('act', lambda: nc.scalar.activation(out=X1[:, :], in_=X2[:, :], func=AF.Exp, scale=-1.0 / 16),
                          reads=['X2'], writes=['X1'])
                    tr.op('act', lambda: nc.scalar.activation(out=X3[:, :], in_=X2[:, :], func=AF.Exp, scale=1.0 / 16),
                          reads=['X2'], writes=['X3'])
                    tr.op('dve', lambda kc=kc: nc.vector.tensor_copy(
                        out=R['dT'][:, kc, 0:32], in_=X1[:, 0:1024].rearrange("p (j s) -> p j s", s=32)[:, :, 31]),
                        reads=['X1'], writes=['dT'])
                    tr.op('dve', lambda kc=kc: nc.vector.tensor_copy(out=R['dT'][:, kc, 32:33], in_=X1[:, T - 1:T]),
                          reads=['X1'], writes=['dT'])
                    proj_fm(wq, wqk, kc * 128, lambda pt, t0, tn, pk, kc=kc: tr.op(
                        'dve', lambda: nc.vector.scalar_tensor_tensor(
                            out=R['qT'][:, kc, t0:t0 + tn], in0=pt[:, 0:tn], scalar=1.0 / 16, in1=X1[:, t0:t0 + tn],
                            op0=ALU.mult, op1=ALU.mult), reads=[pk, 'X1'], writes=['qT']))
                    proj_fm(wkt, wkk, kc * 128, lambda pt, t0, tn, pk, kc=kc: tr.op(
                        'dve', lambda: nc.vector.tensor_tensor(
                            out=R['kT'][:, kc, t0:t0 + tn], in0=pt[:, 0:tn], in1=X3[:, t0:t0 + tn], op=ALU.mult),
                        reads=[pk, 'X3'], writes=['kT']))
                wrel(wqn)
                for i in range(NTT):
                    c0, n = TT[i]
                    pz, pzk = psA()
                    tr.group('pe', [lambda pz=pz, n=n, c0=c0: nc.tensor.matmul(
                        pz[0:n, 0:256], lhsT=aT17[0:17, c0:c0 + n], rhs=wupb[0:17, h * 256:(h + 1) * 256],
                        start=True, stop=True)], reads=['wupb', 'aT17'], writes=[pzk])
                    tr.op('act', lambda pz=pz, n=n: nc.scalar.activation(
                        out=tmpA[0:n, :], in_=pz[0:n, 0:256], func=AF.Exp, scale=-1.0), reads=[pzk], writes=['tmpA'])
                    tr.op('act', lambda n=n: nc.scalar.activation(
                        out=tmpA[0:n, :], in_=tmpA[0:n, :], func=AF.Ln, bias=onec[0:n, 0:1]),
                        reads=['tmpA', 'onec'], writes=['tmpA'])
                    pr, prk = psA()
                    tr.group('pe', [lambda pr=pr, n=n: nc.tensor.matmul(
                        pr[0:n, 0:256], lhsT=revtri[0:n, 0:n], rhs=tmpA[0:n, :], start=True, stop=True)],
                        reads=['tmpA', 'revtri'], writes=[prk])
                    tr.op('act', lambda pr=pr, n=n: nc.scalar.activation(
                        out=tmpB[0:n, :], in_=pr[0:n, 0:256], func=AF.Exp, scale=-1.0 / 16), reads=[prk], writes=['tmpB'])
                    proj_tm(wkt, wkk, i, 256, lambda pt, n, pk, i=i: tr.op(
                        'dve', lambda: nc.vector.tensor_tensor(
                            out=R['khat'][0:n, i, :], in0=pt[0:n, 0:256], in1=tmpB[0:n, :], op=ALU.mult),
                        reads=[pk, 'tmpB'], writes=['khat']))
                wrel(wkn)
                for half in range(2):
                    wv, wvk, wvn = wnext(('glain', o, 8 + 2 * h + half))
                    for i in range(NTT):
                        proj_tm(wv, wvk, i, 256, lambda pt, n, pk, i=i, half=half: tr.op(
                            'act', lambda: nc.scalar.copy(out=R['vtm'][0:n, i, half * 256:(half + 1) * 256],
                                                          in_=pt[0:n, 0:256]), reads=[pk], writes=['vtm']))
                    wrel(wvn)
                glr_head(R, q, ('gla_p', o, h),
                         dr['sgla'][o, h].rearrange("(k p) v -> p k v", p=128),
                         dr['gla_p'][o, h].rearrange("(k p) v -> p k v", p=128),
                         dr['gla_s'][o, h].rearrange("(k p) v -> p k v", p=128))
                sq = qk
                for vc in range(4):
                    tr.op('act', lambda vc=vc: nc.scalar.activation(out=sq[:, vc, :], in_=R['oT'][:, vc, :], func=AF.Square),
                          reads=['oT', 'qT', 'kT'], writes=['sq'])
                for t, (t0, tn) in enumerate(TG):
                    pt, pk = psB()
                    tr.group('pe', [lambda vc=vc, pt=pt, t0=t0, tn=tn: nc.tensor.matmul(
                        pt[:, 0:tn], lhsT=ones_bf[:, :], rhs=sq[:, vc, t0:t0 + tn], start=(vc == 0), stop=(vc == 3))
                        for vc in range(4)], reads=['sq', 'ones'], writes=[pk])
                    tr.op('act', lambda pt=pt, t0=t0, tn=tn: nc.scalar.activation(
                        out=X2[:, t0:t0 + tn], in_=pt[:, 0:tn], func=AF.Sqrt, bias=epsc[:, 0:1], scale=1.0 / 512),
                        reads=[pk, 'epsc'], writes=['X2'])
                tr.op('dve', lambda: nc.vector.reciprocal(out=X2[:, :], in_=X2[:, :]), reads=['X2'], writes=['X2'])
                for half in range(2):
                    wr, wrk, wrn = wnext(('glain', o, 16 + 2 * h + half))
                    for jj in range(2):
                        vc = half * 2 + jj

                        def evac(pt, t0, tn, pk, vc=vc):
                            si = vc % 2
                            tr.op('act', lambda: nc.scalar.activation(out=st2[si][:, 0:tn], in_=pt[:, 0:tn], func=AF.Silu),
                                  reads=[pk], writes=[('st2', si)])
                            tr.op('dve', lambda: nc.vector.scalar_tensor_tensor(
                                out=st2[si][:, 0:tn], in0=st2[si][:, 0:tn], scalar=gnw[:, o * 4 + vc:o * 4 + vc + 1],
                                in1=X2[:, t0:t0 + tn], op0=ALU.mult, op1=ALU.mult),
                                reads=[('st2', si), 'gnw', 'X2'], writes=[('st2', si)])
                            tr.op('dve', lambda: nc.vector.tensor_tensor(
                                out=R['oT'][:, vc, t0:t0 + tn], in0=R['oT'][:, vc, t0:t0 + tn], in1=st2[si][:, 0:tn],
                                op=ALU.mult), reads=[('st2', si), 'oT'], writes=['oT'])
                        proj_fm(wr, wrk, jj * 128, evac)
                    wrel(wrn)
                for rt in range(2):
                    outproj_rows(('glaout', o, 2 * h + rt), R['oT'], 'oT', rt * 2)


        BIGNEG = -30000.0
        SCALE = 128.0 ** -0.5

        def even_layer(l, nidx, q):
            e = l // 2
            has_lb = (e == 1)
            AR.reset()
            rmsnorm(nidx)
            XX = AR([2 * T], F32)
            X1 = XX[:, 0:T]
            X2 = XX[:, T:2 * T]
            X3 = AR([T], F32)
            lbrow = AR([1024], F32)
            omlrow = AR([1024], F32)
            lbT = AR([8], F32)
            omlT = AR([8], F32)
            st2 = [AR([344], F32) for _ in range(2)]
            if has_lb:
                tr.dma('sp', XX[:, 0:2048], dr['lbrow'][:, :], writes=['X1', 'X2'])
                tr.op('dve', lambda: nc.vector.tensor_tensor(out=lbrow[:, :], in0=XX[:, 1024:2048], in1=XX[:, 0:1024],
                                                             op=ALU.subtract), reads=['X1', 'X2'], writes=['lbrow'])
                tr.op('act', lambda: nc.scalar.activation(out=lbrow[:, :], in_=lbrow[:, :], func=AF.Sigmoid),
                      reads=['lbrow'], writes=['lbrow'])
                tr.op('dve', lambda: nc.vector.tensor_scalar(out=omlrow[:, :], in0=lbrow[:, :], scalar1=-1.0, scalar2=1.0,
                                                             op0=ALU.mult, op1=ALU.add), reads=['lbrow'], writes=['omlrow'])
                tr.dma('sp', XX[:, 0:16], dr['lbT'][:, :], reads=['X1', 'X2'], writes=['X1', 'X2'])
                tr.op('dve', lambda: nc.vector.tensor_tensor(out=lbT[:, :], in0=XX[:, 8:16], in1=XX[:, 0:8],
                                                             op=ALU.subtract), reads=['X1', 'X2'], writes=['lbT'])
                tr.op('act', lambda: nc.scalar.activation(out=lbT[:, :], in_=lbT[:, :], func=AF.Sigmoid),
                      reads=['lbT'], writes=['lbT'])
                tr.op('dve', lambda: nc.vector.tensor_scalar(out=omlT[:, :], in0=lbT[:, :], scalar1=-1.0, scalar2=1.0,
                                                             op0=ALU.mult, op1=ALU.add), reads=['lbT'], writes=['omlT'])
            mark = AR.off

            for hp in range(4 if flags.get('ev_hgrn', True) else 0):
                AR.off = mark
                khat2 = AR([NTT, 256], BF16)
                vtm2 = AR([NTT, 256], BF16)
                oTp = AR([2, T], BF16)
                tmpA = AR([256], F32)
                tmpB = AR([256], F32)
                tmpL = AR([256], F32)
                sqb = AR([T], BF16)
                Rs = []
                for hh in range(2):
                    R = {'KC': 1, 'dv': 128}
                    R['qT'] = AR([1, T], BF16)
                    R['kT'] = AR([1, T], BF16)
                    R['oT'] = AR([1, T], BF16)
                    R['dT'] = AR([1, 33], F32)
                    R['S'] = AR([1, 128], F32)
                    R['Sbf'] = AR([1, 128], BF16)
                    R['A'] = AR([32], BF16)
                    R['khat'] = khat2[:, :, hh * 128:(hh + 1) * 128]
                    R['vtm'] = vtm2[:, :, hh * 128:(hh + 1) * 128]
                    Rs.append(R)
                wq, wqk, wqn = wnext(('evin', e, 0 + hp))
                wf, wfk, wfn = wnext(('evin', e, 4 + hp))
                for hh in range(2):
                    R = Rs[hh]
                    ch = hp * 2 + hh
                    proj_fm(wf, wfk, hh * 128, lambda pt, t0, tn, pk: tr.op('act', lambda: nc.scalar.activation(
                        out=X1[:, t0:t0 + tn], in_=pt[:, 0:tn], func=AF.Sigmoid), reads=[pk], writes=['X1']))
                    if has_lb:
                        tr.op('dve', lambda ch=ch: nc.vector.tensor_scalar(
                            out=X1[:, :], in0=X1[:, :], scalar1=omlT[:, ch:ch + 1], scalar2=lbT[:, ch:ch + 1],
                            op0=ALU.mult, op1=ALU.add), reads=['X1', 'lbT', 'omlT'], writes=['X1'])
                    tr.op('act', lambda: nc.scalar.activation(out=X2[:, :], in_=X1[:, :], func=AF.Ln),
                          reads=['X1'], writes=['X2'])
                    tr.op('dve', lambda: nc.vector.tensor_tensor_scan(
                        out=X3[:, :], data0=rmask[:, :], data1=X2[:, :], initial=0.0, op0=ALU.mult, op1=ALU.add),
                        reads=['X2', 'rmask'], writes=['X3'])
                    tr.op('dve', lambda: nc.vector.tensor_scalar(out=X1[:, :], in0=X1[:, :], scalar1=-1.0, scalar2=1.0,
                                                                 op0=ALU.mult, op1=ALU.add), reads=['X1'], writes=['X1'])
                    tr.op('act', lambda: nc.scalar.activation(out=X2[:, :], in_=X3[:, :], func=AF.Exp, scale=-1.0),
                          reads=['X3'], writes=['X2'])
                    tr.op('dve', lambda R=R: nc.vector.tensor_tensor(out=R['kT'][:, 0, :], in0=X1[:, :], in1=X2[:, :],
                                                                     op=ALU.mult), reads=['X1', 'X2'], writes=['kT'])
                    tr.op('act', lambda: nc.scalar.activation(out=X2[:, :], in_=X3[:, :], func=AF.Exp),
                          reads=['X3', 'kT'], writes=['X2'])
                    tr.op('dve', lambda R=R: nc.vector.tensor_copy(
                        out=R['dT'][:, 0, 0:32], in_=X2[:, 0:1024].rearrange("p (j s) -> p j s", s=32)[:, :, 31]),
                        reads=['X2'], writes=['dT'])
                    tr.op('dve', lambda R=R: nc.vector.tensor_copy(out=R['dT'][:, 0, 32:33], in_=X2[:, T - 1:T]),
                          reads=['X2'], writes=['dT'])

                    def evq(pt, t0, tn, pk, R=R):
                        si = (t0 // 344) % 2
                        tr.op('act', lambda: nc.scalar.activation(out=st2[si][:, 0:tn], in_=pt[:, 0:tn], func=AF.Silu),
                              reads=[pk], writes=[('st2', si)])
                        tr.op('dve', lambda: nc.vector.scalar_tensor_tensor(
                            out=R['qT'][:, 0, t0:t0 + tn], in0=st2[si][:, 0:tn], scalar=SCALE, in1=X2[:, t0:t0 + tn],
                            op0=ALU.mult, op1=ALU.mult), reads=[('st2', si), 'X2'], writes=['qT'])
                    proj_fm(wq, wqk, hh * 128, evq)
                wrel(wqn)
                for i in range(NTT):
                    c0, n = TT[i]
                    proj_tm(wf, wfk, i, 256, lambda pt, n, pk: tr.op('act', lambda: nc.scalar.activation(
                        out=tmpA[0:n, :], in_=pt[0:n, 0:256], func=AF.Sigmoid), reads=[pk], writes=['tmpA']))
                    if has_lb:
                        tr.op('dve', lambda n=n: nc.vector.tensor_tensor(
                            out=tmpA[0:n, :], in0=tmpA[0:n, :], in1=omlrow[0:n, hp * 256:(hp + 1) * 256], op=ALU.mult),
                            reads=['tmpA', 'omlrow'], writes=['tmpA'])
                        tr.op('dve', lambda n=n: nc.vector.tensor_tensor(
                            out=tmpA[0:n, :], in0=tmpA[0:n, :], in1=lbrow[0:n, hp * 256:(hp + 1) * 256], op=ALU.add),
                            reads=['tmpA', 'lbrow'], writes=['tmpA'])
                    tr.op('act', lambda n=n: nc.scalar.activation(out=tmpL[0:n, :], in_=tmpA[0:n, :], func=AF.Ln),
                          reads=['tmpA'], writes=['tmpL'])
                    pr, prk = psA()
                    tr.group('pe', [lambda pr=pr, n=n: nc.tensor.matmul(
                        pr[0:n, 0:256], lhsT=revtri[0:n, 0:n], rhs=tmpL[0:n, :], start=True, stop=True)],
                        reads=['tmpL', 'revtri'], writes=[prk])
                    tr.op('act', lambda pr=pr, n=n: nc.scalar.activation(out=tmpB[0:n, :], in_=pr[0:n, 0:256], func=AF.Exp),
                          reads=[prk], writes=['tmpB'])
                    tr.op('dve', lambda n=n: nc.vector.tensor_scalar(out=tmpA[0:n, :], in0=tmpA[0:n, :], scalar1=-1.0,
                                                                     scalar2=1.0, op0=ALU.mult, op1=ALU.add),
                          reads=['tmpA'], writes=['tmpA'])
                    tr.op('dve', lambda n=n, i=i: nc.vector.tensor_tensor(out=khat2[0:n, i, :], in0=tmpA[0:n, :],
                                                                          in1=tmpB[0:n, :], op=ALU.mult),
                          reads=['tmpA', 'tmpB'], writes=['khat'])
                wrel(wfn)
                wi, wik, win = wnext(('evin', e, 8 + hp))
                for i in range(NTT):
                    proj_tm(wi, wik, i, 256, lambda pt, n, pk, i=i: tr.op('act', lambda: nc.scalar.copy(
                        out=vtm2[0:n, i, :], in_=pt[0:n, 0:256]), reads=[pk], writes=['vtm']))
                wrel(win)
                for hh in range(2):
                    hd = hp * 2 + hh
                    glr_head(Rs[hh], q, ('hgrn_p', e, hd),
                             dr['shgrn'][e, hd].rearrange("(k p) v -> p k v", p=128),
                             dr['hgrn_p'][e, hd].rearrange("(k p) v -> p k v", p=128),
                             dr['hgrn_s'][e, hd].rearrange("(k p) v -> p k v", p=128))
                wg, wgk, wgn = wnext(('evin', e, 12 + hp))
                for hh in range(2):
                    R = Rs[hh]
                    tr.op('act', lambda R=R: nc.scalar.activation(out=sqb[:, :], in_=R['oT'][:, 0, :], func=AF.Square),
                          reads=['oT'], writes=['sqb'])
                    for t, (t0, tn) in enumerate(TG):
                        pt, pk = psB()
                        tr.group('pe', [lambda pt=pt, t0=t0, tn=tn: nc.tensor.matmul(
                            pt[:, 0:tn], lhsT=ones_bf[:, :], rhs=sqb[:, t0:t0 + tn], start=True, stop=True)],
                            reads=['sqb', 'ones'], writes=[pk])
                        tr.op('act', lambda pt=pt, t0=t0, tn=tn: nc.scalar.activation(
                            out=X2[:, t0:t0 + tn], in_=pt[:, 0:tn], func=AF.Sqrt, bias=epsc[:, 0:1], scale=1.0 / 128),
                            reads=[pk, 'epsc'], writes=['X2'])
                    tr.op('dve', lambda: nc.vector.reciprocal(out=X2[:, :], in_=X2[:, :]), reads=['X2'], writes=['X2'])

                    def evg(pt, t0, tn, pk, R=R, hh=hh):
                        si = (t0 // 344) % 2
                        tr.op('act', lambda: nc.scalar.activation(out=st2[si][:, 0:tn], in_=pt[:, 0:tn], func=AF.Silu),
                              reads=[pk], writes=[('st2', si)])
                        tr.op('dve', lambda: nc.vector.scalar_tensor_tensor(
                            out=st2[si][:, 0:tn], in0=st2[si][:, 0:tn], scalar=hnw[:, e:e + 1], in1=X2[:, t0:t0 + tn],
                            op0=ALU.mult, op1=ALU.mult), reads=[('st2', si), 'hnw', 'X2'], writes=[('st2', si)])
                        tr.op('dve', lambda: nc.vector.tensor_tensor(
                            out=oTp[:, hh, t0:t0 + tn], in0=R['oT'][:, 0, t0:t0 + tn], in1=st2[si][:, 0:tn], op=ALU.mult),
                            reads=[('st2', si), 'oT'], writes=['oTp'])
                    proj_fm(wg, wgk, hh * 128, evg)
                wrel(wgn)
                outproj_rows(('evout', e, hp), oTp, 'oTp', 0)

            AR.reset()
            oTm = AR([8, T], BF16)
            qs32 = AR([8, 8], F32)
            ks32 = AR([8, 8], F32)
            vs32 = AR([1024], F32)
            smark = AR.off
            qTh = AR([2, T], BF16)
            kTh = AR([2, T], BF16)
            KTa = AR([4 * TP], BF16)
            Va = AR([32, 128], BF16)
            stg = [AR([256], F32) for _ in range(2)]
            PT = [AR([256], BF16) for _ in range(3)]
            biasT = AR([256], BF16)
            scp = AR([24], F32)
            mx8 = AR([8], F32)
            bq = AR([16], F32)
            kms = AR([4], F32)
            rden = AR([256], F32)
            tr.op('pool', lambda: nc.gpsimd.memset(biasT[:, :], 0.0), writes=['biasT'])
            tr.op('pool', lambda: nc.gpsimd.memset(oTm[:], 0.0), writes=['oTm'])
            for mp in range(4 if flags.get('ev_moba', True) else 0):
                wmq, wmqk, wmqn = wnext(('evin', e, 16 + mp))
                for hh in range(2):
                    def evmq(pt, t0, tn, pk, hh=hh):
                        tr.op('act', lambda: nc.scalar.copy(out=qTh[:, hh, t0:t0 + tn], in_=pt[:, 0:tn]),
                              reads=[pk], writes=['qTh'])
                        if t0 == 688 and q == 0 and flags.get('mb_qs', True):
                            tr.op('dve', lambda: nc.vector.tensor_copy(out=qs32[:, mp * 2 + hh, :], in_=qTh[:, hh, TP:T]),
                                  reads=['qTh'], writes=['qs32'])
                    proj_fm(wmq, wmqk, hh * 128, evmq)
                wrel(wmqn)
                wmk, wmkk, wmkn = wnext(('evin', e, 20 + mp))
                for hh in range(2):
                    def evmk(pt, t0, tn, pk, hh=hh):
                        tr.op('act', lambda: nc.scalar.copy(out=kTh[:, hh, t0:t0 + tn], in_=pt[:, 0:tn]),
                              reads=[pk], writes=['kTh'])
                        if t0 == 688 and q == 0 and flags.get('mb_qs', True):
                            tr.op('dve', lambda: nc.vector.tensor_copy(out=ks32[:, mp * 2 + hh, :], in_=kTh[:, hh, TP:T]),
                                  reads=['kTh'], writes=['ks32'])
                    proj_fm(wmk, wmkk, hh * 128, evmk)

                def kv_out(w_, wk_, name, sname):
                    for i in range(NTT if flags.get('mb_kvout', True) else 0):
                        c0, n = TT[i]
                        si = i % 2

                        def ev(pt, n, pk, i=i, c0=c0, si=si):
                            tr.op('act', lambda: nc.scalar.copy(out=stg[si][0:n, :], in_=pt[0:n, 0:256]),
                                  reads=[pk], writes=[('stg', si)])
                            if i < NTT - 1:
                                tr.dma('sp', dr[name][e, q * TP + c0:q * TP + c0 + n, mp * 256:(mp + 1) * 256],
                                       stg[si][0:n, :], reads=[('stg', si)], writes=[(name, mp)], is_out=True)
                            elif q == 0:
                                tr.dma('sp', dr[sname][e, :, mp * 256:(mp + 1) * 256], stg[si][0:n, :],
                                       reads=[('stg', si)], is_out=True)
                                if name == 'vp':
                                    tr.op('dve', lambda: nc.vector.tensor_copy(
                                        out=vs32[0:TS, mp * 256:(mp + 1) * 256], in_=stg[si][0:TS, :]),
                                        reads=[('stg', si)], writes=['vs32'])
                        proj_tm(w_, wk_, i, 256, ev)
                kv_out(wmk, wmkk, 'kp', 'ks')
                wrel(wmkn)
                wmv, wmvk, wmvn = wnext(('evin', e, 24 + mp))
                kv_out(wmv, wmvk, 'vp', 'vs')
                wrel(wmvn)
                for hh in range(2 if flags.get('mb_ktd', True) else 0):
                    hd = mp * 2 + hh
                    tr.dma('sp', KTd[hd][:, q * TP:(q + 1) * TP], kTh[:, hh, 0:TP], reads=['kTh'], writes=[('KTd', hd)])
                    tr.op('dve', lambda hh=hh: nc.vector.tensor_reduce(
                        out=kms[:, :], in_=kTh[:, hh, 0:TP].rearrange("p (b s) -> p b s", s=256), axis=AX.X, op=ALU.add),
                        reads=['kTh'], writes=['kms'])
                    tr.op('dve', lambda hd=hd: nc.vector.tensor_scalar(
                        out=kmT[:, hd, 4 * q:4 * q + 4], in0=kms[:, :], scalar1=1.0 / 256, scalar2=None, op0=ALU.mult),
                        reads=['kms'], writes=['kmT'])
                nk = (q + 1) * TP
                for hh in range(2 if flags.get('moba_attn', True) else 0):
                    hd = mp * 2 + hh
                    tr.dma('sp', KTa[:, 0:nk], KTd[hd][:, 0:nk], reads=[('KTd', hd)], writes=['KTa'])
                    tr.dma('pool', Va[:, 0:nk // 128, :],
                           dr['vp'][e][0:nk, hd * 128:(hd + 1) * 128].rearrange("(t p) d -> p t d", p=128),
                           reads=[('vp', mp)], writes=['Va'])
                    for Bl in range(4):
                        B = 4 * q + Bl
                        qc0 = Bl * 256
                        use_sel = B >= 4
                        if use_sel:
                            for qt in range(2):
                                tr.group('pe', [lambda qt=qt: nc.tensor.matmul(
                                    ps[6][:, 0:16], lhsT=qTh[:, hh, qc0 + qt * 128:qc0 + (qt + 1) * 128],
                                    rhs=kmT[:, hd, 0:16], start=True, stop=True)], reads=['qTh', 'kmT'], writes=[('ps', 6)])
                                tr.op('dve', lambda: nc.vector.tensor_copy(out=scp[:, :], in_=negc[:, :]),
                                      reads=['negc'], writes=['scp'])
                                tr.op('dve', lambda: nc.vector.tensor_copy(out=scp[:, 0:B], in_=ps[6][:, 0:B]),
                                      reads=[('ps', 6), 'scp'], writes=['scp'])
                                tr.op('dve', lambda: nc.vector.max(out=mx8[:, :], in_=scp[:, :]), reads=['scp'], writes=['mx8'])
                                tr.op('dve', lambda: nc.vector.tensor_scalar(
                                    out=bq[:, :], in0=scp[:, 0:16], scalar1=mx8[:, 2:3], scalar2=1.0,
                                    op0=ALU.is_ge, op1=ALU.subtract), reads=['scp', 'mx8'], writes=['bq'])
                                tr.op('dve', lambda: nc.vector.tensor_scalar(
                                    out=bq[:, :], in0=bq[:, :], scalar1=-BIGNEG, scalar2=None, op0=ALU.mult),
                                    reads=['bq'], writes=['bq'])
                                tr.op('pe', lambda: nc.tensor.transpose(out=ps[7][0:16, 0:128], in_=bq[:, :],
                                                                        identity=ident[:, :]),
                                      reads=['bq', 'ident'], writes=[('ps', 7)])
                                tr.op('act', lambda qt=qt: nc.scalar.copy(out=biasT[0:16, qt * 128:(qt + 1) * 128],
                                                                         in_=ps[7][0:16, 0:128]),
                                      reads=[('ps', 7), 'biasT'], writes=['biasT'])
                        ktiles = []
                        for n_ in range(B):
                            ktiles.append((2 * n_, n_, None))
                            ktiles.append((2 * n_ + 1, n_, None))
                        ktiles.append((2 * B, None, mskA))
                        ktiles.append((2 * B + 1, None, mskB))
                        for ki, (kt, nb, msk) in enumerate(ktiles):
                            pS, pSk = psA()
                            fns = [lambda pS=pS, kt=kt: nc.tensor.matmul(
                                pS[:, 0:256], lhsT=KTa[:, kt * 128:(kt + 1) * 128], rhs=qTh[:, hh, qc0:qc0 + 256],
                                start=True, stop=not (use_sel and nb is not None))]
                            rd = ['KTa', 'qTh']
                            if use_sel and nb is not None:
                                fns.append(lambda pS=pS, nb=nb: nc.tensor.matmul(
                                    pS[:, 0:256], lhsT=esel[0:32, nb, :], rhs=biasT[0:32, :], start=False, stop=True))
                                rd += ['esel', 'biasT']
                            tr.group('pe', fns, reads=rd, writes=[pSk])
                            pt_ = PT[ki % 3]
                            pk_ = ('PT', ki % 3)
                            tr.op('act', lambda pS=pS, pt_=pt_: nc.scalar.activation(
                                out=pt_[:, :], in_=pS[:, 0:256], func=AF.Exp, scale=SCALE), reads=[pSk], writes=[pk_])
                            if msk is not None:
                                tr.op('dve', lambda pt_=pt_, msk=msk: nc.vector.tensor_tensor(
                                    out=pt_[:, :], in0=pt_[:, :], in1=msk[:, :], op=ALU.mult),
                                    reads=[pk_, 'mskA', 'mskB'], writes=[pk_])
                            first = (ki == 0)
                            last = (ki == len(ktiles) - 1)
                            tr.group('pe', [lambda pt_=pt_, kt=kt: nc.tensor.matmul(
                                ps[4][:, 0:256], lhsT=Va[:, kt, :], rhs=pt_[:, :], start=first, stop=last)],
                                reads=['Va', pk_], writes=[('ps', 4)])
                            tr.group('pe', [lambda pt_=pt_: nc.tensor.matmul(
                                ps[5][:, 0:256], lhsT=ones_bf[:, :], rhs=pt_[:, :], start=first, stop=last)],
                                reads=['ones', pk_], writes=[('ps', 5)])
                        tr.op('dve', lambda: nc.vector.reciprocal(out=rden[:, :], in_=ps[5][:, 0:256]),
                              reads=[('ps', 5)], writes=['rden'])
                        tr.op('dve', lambda hd=hd: nc.vector.tensor_tensor(
                            out=oTm[:, hd, qc0:qc0 + 256], in0=ps[4][:, 0:256], in1=rden[:, :], op=ALU.mult),
                            reads=[('ps', 4), 'rden'], writes=['oTm'])
            if q == 0 and flags.get('sample_moba', SAMPLE_MOBA_DEFAULT):
                moba_sample(e, oTm, qs32, ks32, vs32, smark)
            else:
                tr.op('pool', lambda: nc.gpsimd.memset(oTm[:, :, TP:T], 0.0), reads=['oTm'], writes=['oTm'])
            for mp in range(4 if flags.get('ev_moba', True) else 0):
                outproj_rows(('evout', e, 4 + mp), oTm, 'oTm', 2 * mp)

        def moba_sample(e, oTm, qs32, ks32, vs32, smark):
            tr.barrier()
            AR.off = smark
            ptab_f = AR([128], F32)
            ptab_i = ptab_f.bitcast(I32)
            ptf = AR([128], F32)
            idx_f = AR([128], F32)
            idx = idx_f.bitcast(I32)
            iot_f = AR([1], F32)
            iot_i = iot_f.bitcast(I32)
            iotf = AR([1], F32)
            Kp = [AR([1024], F32) for _ in range(2)]
            KTp = AR([4, 128], F32)
            ksum = AR([4, 128], F32)
            ksb = AR([4, 64], F32)
            Ls = AR([128, 32], F32)
            sc = AR([4, 64], F32)
            mx = AR([8], F32)
            biasS = AR([4, 64], F32)
            Dg = AR([64, 8], F32)
            Pown = AR([4, 8], F32)
            ones32 = AR([128], F32)
            rdn = AR([32], F32)
            ck = dr['cache_k'].rearrange("e r c -> (e r) c")
            cv = dr['cache_v'].rearrange("e r c -> (e r) c")
            tr.op('pool', lambda: nc.gpsimd.memset(ones32[:, :], 1.0), writes=['ones32'])
            tr.dma('sp', ptab_i[:, :], dr['ptab'][:, :], writes=['ptab'])
            tr.op('pool', lambda: nc.gpsimd.iota(out=iot_i[:, :], pattern=[[0, 1]], base=0, channel_multiplier=1),
                  writes=['iot'])
            tr.op('dve', lambda: nc.vector.tensor_copy(out=iotf[:, :], in_=iot_i[:, :]), reads=['iot'], writes=['iotf'])
            tr.op('dve', lambda: nc.vector.tensor_copy(out=ptf[:, :], in_=ptab_i[:, :]), reads=['ptab'], writes=['ptf'])
            tr.op('dve', lambda: nc.vector.tensor_scalar(out=ptf[:, :], in0=ptf[:, :], scalar1=128.0, scalar2=iotf[:, 0:1],
                                                         op0=ALU.mult, op1=ALU.add), reads=['ptf', 'iotf'], writes=['ptf'])
            tr.op('dve', lambda: nc.vector.tensor_scalar(out=ptf[:, :], in0=ptf[:, :], scalar1=float(e * 1280 * 128), scalar2=None,
                                                         op0=ALU.add), reads=['ptf'], writes=['ptf'])
            tr.op('dve', lambda: nc.vector.tensor_copy(out=idx[:, :], in_=ptf[:, :]), reads=['ptf'], writes=['idx'])

            if 'bc' not in K.__dict__:
                K.bc = nc.gpsimd.to_reg(2 * 1280 * 128 - 1)

            def gather(src2d, p):
                buf = Kp[p % 2]
                bk = ('Kp', p % 2)
                tr.dma('pool', None, None, reads=['idx'], writes=[bk], fn=lambda: nc.gpsimd.indirect_dma_start(
                    out=buf[:, :], out_offset=None, in_=src2d[:, :],
                    in_offset=bass.IndirectOffsetOnAxis(ap=idx[:, p:p + 1], axis=0),
                    bounds_check=K.bc, oob_is_err=False))
                return buf, bk

            for hh2 in range(2):
                for p in range(128):
                    buf, bk = gather(ck, p)
                    pT, pTk = psA()
                    for h4 in range(4):
                        hd = hh2 * 4 + h4
                        tr.op('pe', lambda h4=h4, hd=hd, pT=pT, buf=buf: nc.tensor.transpose(
                            out=pT[:, h4 * 128:(h4 + 1) * 128], in_=buf[:, hd * 128:(hd + 1) * 128], identity=ident[:, :]),
                            reads=[bk, 'ident'], writes=[pTk])
                    tr.op('act', lambda pT=pT: nc.scalar.copy(out=KTp[:, :, :],
                                                             in_=pT[:, :].rearrange("p (h k) -> p h k", h=4)),
                          reads=[pTk], writes=['KTp'])
                    tr.op('dve', lambda p=p: nc.vector.tensor_reduce(
                        out=ksum[:, :, p], in_=KTp[:, :, :], axis=AX.X, op=ALU.add),
                        reads=['KTp'], writes=['ksum'])
                    pL, pLk = psA()
                    for h4 in range(4):
                        hd = hh2 * 4 + h4
                        tr.op('pe', lambda h4=h4, hd=hd, pL=pL: nc.tensor.matmul(
                            pL[:, h4 * 8:(h4 + 1) * 8], lhsT=KTp[:, h4, :], rhs=qs32[:, hd, :], start=True, stop=True),
                            reads=['KTp', 'qs32'], writes=[pLk])
                    tr.op('dve', lambda pL=pL, p=p: nc.vector.tensor_copy(out=Ls[:, p, :], in_=pL[:, 0:32]),
                          reads=[pLk], writes=['Ls'])
                if flags.get('sm_stage', 4) < 2:
                    continue
                ks4 = ksum[:, :, :].rearrange("p h (n two) -> p h n two", two=2)
                tr.op('dve', lambda: nc.vector.tensor_tensor(out=ksb[:, :, :], in0=ks4[:, :, :, 0], in1=ks4[:, :, :, 1],
                                                             op=ALU.add), reads=['ksum'], writes=['ksb'])
                pS, pSk = psA()
                for h4 in range(4):
                    hd = hh2 * 4 + h4
                    tr.op('pe', lambda h4=h4, hd=hd, pS=pS: nc.tensor.matmul(
                        pS[0:TS, h4 * 64:(h4 + 1) * 64], lhsT=qs32[:, hd, :], rhs=ksb[:, h4, :], start=True, stop=True),
                        reads=['ksb', 'qs32'], writes=[pSk])
                tr.op('dve', lambda pS=pS: nc.vector.tensor_copy(out=sc[0:TS, :, :],
                                                                 in_=pS[0:TS, 0:256].rearrange("p (h n) -> p h n", h=4)),
                      reads=[pSk], writes=['sc'])
                for h4 in range(4):
                    tr.op('dve', lambda h4=h4: nc.vector.max(out=mx[0:TS, :], in_=sc[0:TS, h4, :]), reads=['sc'], writes=['mx'])
                    tr.op('dve', lambda h4=h4: nc.vector.tensor_scalar(
                        out=biasS[0:TS, h4, :], in0=sc[0:TS, h4, :], scalar1=mx[0:TS, 2:3], scalar2=1.0,
                        op0=ALU.is_ge, op1=ALU.subtract), reads=['sc', 'mx'], writes=['biasS'])
                tr.op('dve', lambda: nc.vector.tensor_scalar(out=biasS[0:TS, :, :], in0=biasS[0:TS, :, :], scalar1=-BIGNEG,
                                                             scalar2=None, op0=ALU.mult), reads=['biasS'], writes=['biasS'])
                Ls5 = Ls[:, :, :].rearrange("p (n two) (h q) -> p n two h q", two=2, h=4)
                for h4 in range(4):
                    for qi in range(TS):
                        tr.op('dve', lambda h4=h4, qi=qi: nc.vector.tensor_scalar(
                            out=Dg[0:TS, :, qi], in0=biasS[0:TS, h4, :], scalar1=ident[0:TS, qi:qi + 1], scalar2=None,
                            op0=ALU.mult), reads=['biasS', 'ident'], writes=['Dg'])
                    pB, pBk = psA()
                    for half in range(2):
                        tr.op('pe', lambda pB=pB, half=half: nc.tensor.matmul(
                            pB[:, half * 256:(half + 1) * 256], lhsT=ones32[0:TS, 0:128],
                            rhs=Dg[0:TS, half * 32:(half + 1) * 32, :].rearrange("p n q -> p (n q)"),
                            start=True, stop=True), reads=['Dg', 'ones32'], writes=[pBk])
                    for pp in range(2):
                        tr.op('dve', lambda h4=h4, pp=pp, pB=pB: nc.vector.tensor_tensor(
                            out=Ls5[:, :, pp, h4, :], in0=Ls5[:, :, pp, h4, :],
                            in1=pB[:, 0:512].rearrange("p (n q) -> p n q", q=TS), op=ALU.add),
                            reads=[pBk, 'Ls'], writes=['Ls'])
                tr.op('act', lambda: nc.scalar.activation(out=Ls[:, :, :], in_=Ls[:, :, :], func=AF.Exp, scale=SCALE),
                      reads=['Ls'], writes=['Ls'])
                pO, pOk = psA()
                for h4 in range(4):
                    hd = hh2 * 4 + h4
                    tr.op('pe', lambda h4=h4, hd=hd, pO=pO: nc.tensor.matmul(
                        pO[0:TS, h4 * 8:(h4 + 1) * 8], lhsT=ks32[:, hd, :], rhs=qs32[:, hd, :], start=True, stop=True),
                        reads=['ks32', 'qs32'], writes=[pOk])
                tr.op('act', lambda pO=pO: nc.scalar.activation(
                    out=Pown[0:TS, :, :], in_=pO[0:TS, 0:32].rearrange("p (h q) -> p h q", h=4), func=AF.Exp, scale=SCALE),
                    reads=[pOk], writes=['Pown'])
                for h4 in range(4):
                    tr.op('dve', lambda h4=h4: nc.vector.tensor_tensor(
                        out=Pown[0:TS, h4, :], in0=Pown[0:TS, h4, :], in1=cmask[0:TS, 0:TS], op=ALU.mult),
                        reads=['Pown', 'cmask'], writes=['Pown'])
                if flags.get('sm_stage', 4) < 3:
                    continue
                pD, pDk = psA()
                fns = [lambda p=p, pD=pD: nc.tensor.matmul(pD[:, 0:32], lhsT=ones32[:, 0:128], rhs=Ls[:, p, :],
                                                           start=(p == 0), stop=False) for p in range(128)]
                fns.append(lambda pD=pD: nc.tensor.matmul(
                    pD[:, 0:32], lhsT=ones32[0:TS, 0:128], rhs=Pown[0:TS, :, :].rearrange("p h q -> p (h q)"),
                    start=False, stop=True))
                tr.group('pe', fns, reads=['Ls', 'Pown', 'ones32'], writes=[pDk])
                tr.op('dve', lambda pD=pD: nc.vector.reciprocal(out=rdn[:, :], in_=pD[:, 0:32]), reads=[pDk], writes=['rdn'])
                if flags.get('sm_stage', 4) < 4:
                    continue
                for p in range(128):
                    buf, bk = gather(cv, p)
                    for h4 in range(4):
                        hd = hh2 * 4 + h4
                        tr.op('pe', lambda h4=h4, hd=hd, buf=buf, p=p: nc.tensor.matmul(
                            ps[4 + h4][:, 0:TS], lhsT=buf[:, hd * 128:(hd + 1) * 128], rhs=Ls[:, p, h4 * 8:(h4 + 1) * 8],
                            start=(p == 0), stop=False), reads=[bk, 'Ls'], writes=[('ps', 4 + h4)])
                for h4 in range(4):
                    hd = hh2 * 4 + h4
                    tr.op('pe', lambda h4=h4, hd=hd: nc.tensor.matmul(
                        ps[4 + h4][:, 0:TS], lhsT=vs32[0:TS, hd * 128:(hd + 1) * 128], rhs=Pown[0:TS, h4, :],
                        start=False, stop=True), reads=['vs32', 'Pown'], writes=[('ps', 4 + h4)])
                    tr.op('dve', lambda h4=h4, hd=hd: nc.vector.tensor_tensor(
                        out=oTm[:, hd, TP:T], in0=ps[4 + h4][:, 0:TS], in1=rdn[:, h4 * 8:(h4 + 1) * 8], op=ALU.mult),
                        reads=[('ps', 4 + h4), 'rdn'], writes=['oTm'])

        prog = flags.get('prog', DEFAULT_PROG)
        for li, layer in enumerate(prog):
            for q in range(flags.get('nq', 4)):
                if li == 0:
                    load_x(q)
                else:
                    load_xd(q)
                for (kind, l) in layer:
                    if kind == 'ffn0':
                        ffn(l, 0, l * 3 + 0)
                    elif kind == 'ffn1':
                        ffn(l, 1, l * 3 + 2)
                    elif kind == 'gla':
                        gla_layer(l, l * 3 + 1, q)
                    elif kind == 'even':
                        even_layer(l, l * 3 + 1, q)
                if li == len(prog) - 1:
                    final_out(q)
                else:
                    store_xd(q)
        tr.finish()
    return nc


def _host_inputs(inp, c, flags=None):
    inp = _Lazy(inp)
    b, j = c // 4, c % 4
    norm_all = np.concatenate([inp['norm_w'].reshape(12, D), inp['final_norm_w'].reshape(1, D)], axis=0)
    normT = np.ascontiguousarray(norm_all.reshape(13, NCH, 128).transpose(2, 0, 1).reshape(128, 13 * NCH))
    m = {
        'xp': np.ascontiguousarray(inp['x_prompt'][b]),
        'xs': np.ascontiguousarray(inp['x_sample'][c]),
        'normT': normT,
        'ffn_gate': inp['ffn_gate'], 'ffn_up': inp['ffn_up'], 'ffn_down': inp['ffn_down'],
        'gla_w_in': inp['gla_w_in'], 'gla_w_gate_up': inp['gla_w_gate_up'],
        'gla_b_gate': np.ascontiguousarray(inp['gla_b_gate'].reshape(2, 1, 1024)), 'gla_w_out': inp['gla_w_out'],
        'gnormT': np.ascontiguousarray(inp['gla_norm_w'].reshape(2, 4, 128).transpose(2, 0, 1).reshape(128, 8)),
        'sgla': np.ascontiguousarray(inp['state_gla'][:, c]),
        'even_w_in': inp['even_w_in'], 'even_w_out': inp['even_w_out'],
        'lbrow': np.ascontiguousarray(np.tile(inp['hgrn_lb_logits'].reshape(1, 2048), (128, 1))),
        'lbT': np.ascontiguousarray(inp['hgrn_lb_logits'].reshape(2, 8, 128).transpose(2, 0, 1).reshape(128, 16)),
        'hnormT': np.ascontiguousarray(inp['hgrn_norm_w'].T),
        'shgrn': np.ascontiguousarray(inp['state_hgrn'][:, c]),
        'cache_k': inp['cache_k'].reshape(2, 1280 * 128, 1024),
        'cache_v': inp['cache_v'].reshape(2, 1280 * 128, 1024),
        'ptab': np.ascontiguousarray(np.tile(inp['page_table'][c].reshape(1, 128).astype(np.int32), (128, 1))),
    }
    m = {k: v for k, v in m.items() if isinstance(v, np.ndarray) and v.dtype != object}
    if not (flags or {}).get('sample_moba', SAMPLE_MOBA_DEFAULT):
        for k in ('cache_k', 'cache_v', 'ptab'):
            m.pop(k, None)
    m.setdefault('gnormT', np.zeros((128, 8), np.float32))
    m.setdefault('hnormT', np.zeros((128, 2), np.float32))
    return m


def run(inp, flags, trace=False):
    nc = build(flags)
    in_maps = [_host_inputs(inp, c, flags) for c in range(8)]
    res = run_bass_kernel_spmd(nc, in_maps, core_ids=list(range(8)), trace=trace)
    return res


def kernel(**inp):
    inp = {k: np.asarray(v) for k, v in inp.items()}
    res = run(inp, {})
    r = res.results
    yp = np.stack([r[0]['yp'], r[4]['yp']])
    ys = np.stack([r[c]['ys'] for c in range(8)])
    kp = np.stack([r[0]['kp'], r[4]['kp']], axis=1).reshape(2, 2, 32, 128, 8, 128)
    vp = np.stack([r[0]['vp'], r[4]['vp']], axis=1).reshape(2, 2, 32, 128, 8, 128)
    hp = np.stack([r[0]['hgrn_p'], r[4]['hgrn_p']], axis=1)
    gp = np.stack([r[0]['gla_p'], r[4]['gla_p']], axis=1)
    ks = np.stack([r[c]['ks'] for c in range(8)], axis=1).reshape(2, 8, 8, 8, 128)
    vs = np.stack([r[c]['vs'] for c in range(8)], axis=1).reshape(2, 8, 8, 8, 128)
    hs = np.stack([r[c]['hgrn_s'] for c in range(8)], axis=1)
    gs = np.stack([r[c]['gla_s'] for c in range(8)], axis=1)
    return (yp, ys, kp, vp, hp, gp, ks, vs, hs, gs)
```
